# Optimizing a Trainium2 kernel written in Bass

```python
import jax, jax.numpy as jnp
from jax import lax
import numpy as np

D_MODEL = 1024
BATCH = 16
SEQ = 2048
DEPTH = 2

CHUNK = 64
Q_BLOCK = 128
EPS = 1e-6
NEG = -1e30

A_HEADS = 6
A_NOPE = 64
A_ROPE = 32
A_V = 64
A_Q_RANK = 384
A_KV_RANK = 256
A_WIDTH = A_HEADS * A_V
ROPE_THETA = 10000.0

B_HEADS = 5
B_HD = 64
B_WIDTH = B_HEADS * B_HD
B_LEFT_CHUNKS = 8
B_BAND = (B_LEFT_CHUNKS + 1) * CHUNK
REL_CLIP = 128

C_HEADS = 5
C_HD = 64
C_WIDTH = C_HEADS * C_HD
FORGET_BIAS_INIT = 2.0

D_MIX = A_WIDTH + B_WIDTH + C_WIDTH

IN_SIZES = (A_Q_RANK, A_KV_RANK, A_ROPE, A_WIDTH,
            B_WIDTH, B_WIDTH, B_WIDTH, B_WIDTH,
            C_WIDTH, C_WIDTH, C_WIDTH, C_HEADS, C_WIDTH)
N_IN = (A_Q_RANK + A_KV_RANK + A_ROPE + A_WIDTH + 4 * B_WIDTH + 4 * C_WIDTH + C_HEADS)

kernel_name = "hybrid_mla_chunkband_fox_encoder"


def rmsnorm(x, g):
    xf = x.astype(jnp.float32)
    y = xf * lax.rsqrt(jnp.mean(xf * xf, axis=-1, keepdims=True) + EPS)
    return (y * g.astype(jnp.float32)).astype(x.dtype)


def split_cols(z, sizes):
    idx, o = [], 0
    for s in sizes[:-1]:
        o += s
        idx.append(o)
    return jnp.split(z, idx, axis=-1)


def rope_tables(positions):
    inv = ROPE_THETA ** (-jnp.arange(0, A_ROPE, 2, dtype=jnp.float32) / A_ROPE)
    ang = positions.astype(jnp.float32)[..., None] * inv
    return jnp.cos(ang), jnp.sin(ang)


def apply_rope(x, cos, sin):
    x1, x2 = jnp.split(x.astype(jnp.float32), 2, axis=-1)
    out = jnp.concatenate([x1 * cos - x2 * sin, x1 * sin + x2 * cos], axis=-1)
    return out.astype(x.dtype)


def attend(s, mask, v):
    p = jax.nn.softmax(jnp.where(mask, s, NEG), axis=-1)
    return jnp.einsum('bhqk,bhkd->bhqd', p.astype(v.dtype), v)


def mla_mixer(c_q, c_kv, k_pe, q_norm_g, w_uq, kv_norm_g, w_ukv, cos, sin):
    Bn, S, _ = c_q.shape
    q = (rmsnorm(c_q, q_norm_g) @ w_uq).reshape(Bn, S, A_HEADS, A_NOPE + A_ROPE)
    q_nope, q_pe = q[..., :A_NOPE], q[..., A_NOPE:]
    q_pe = apply_rope(q_pe, cos[:, :, None, :], sin[:, :, None, :])
    kv = (rmsnorm(c_kv, kv_norm_g) @ w_ukv).reshape(Bn, S, A_HEADS, A_NOPE + A_V)
    k_nope, v = kv[..., :A_NOPE], kv[..., A_NOPE:]
    k_pe = apply_rope(k_pe, cos, sin)
    k_pe = jnp.broadcast_to(k_pe[:, :, None, :], (Bn, S, A_HEADS, A_ROPE))
    q = jnp.concatenate([q_nope, q_pe], axis=-1).transpose(0, 2, 1, 3)
    k = jnp.concatenate([k_nope, k_pe], axis=-1).transpose(0, 2, 1, 3)
    v = v.transpose(0, 2, 1, 3)
    scale = (A_NOPE + A_ROPE) ** -0.5
    outs = []
    for i in range(S // Q_BLOCK):
        q0 = i * Q_BLOCK
        kend = q0 + Q_BLOCK
        s = jnp.einsum('bhqd,bhkd->bhqk', q[:, :, q0:kend], k[:, :, :kend],
                       preferred_element_type=jnp.float32) * scale
        q_chunk = (q0 + jnp.arange(Q_BLOCK)) // CHUNK
        k_chunk = jnp.arange(kend) // CHUNK
        mask = k_chunk[None, :] <= q_chunk[:, None]
        outs.append(attend(s, mask, v[:, :, :kend]))
    o = jnp.concatenate(outs, axis=2)
    return o.transpose(0, 2, 1, 3).reshape(Bn, S, A_WIDTH)


def chunk_band_mixer(q, k, v, rel_bias):
    Bn, S, _ = q.shape
    NC = S // CHUNK
    qc = q.reshape(Bn, NC, CHUNK, B_HEADS, B_HD)

    def band(t):
        t = t.reshape(Bn, S, B_HEADS, B_HD)
        t = jnp.pad(t, ((0, 0), (B_LEFT_CHUNKS * CHUNK, 0), (0, 0), (0, 0)))
        t = t.reshape(Bn, NC + B_LEFT_CHUNKS, CHUNK, B_HEADS, B_HD)
        t = jnp.stack([t[:, j:j + NC] for j in range(B_LEFT_CHUNKS + 1)], axis=2)
        return t.reshape(Bn, NC, B_BAND, B_HEADS, B_HD)

    kb, vb = band(k), band(v)
    s = jnp.einsum('bnqhd,bnkhd->bnhqk', qc, kb,
                   preferred_element_type=jnp.float32) * (B_HD ** -0.5)
    rel = (B_LEFT_CHUNKS * CHUNK + jnp.arange(CHUNK))[:, None] - jnp.arange(B_BAND)[None, :]
    idx = jnp.clip(rel, -REL_CLIP, REL_CLIP) + REL_CLIP
    s = s + rel_bias[:, idx].astype(jnp.float32)
    k_chunk = jnp.arange(NC)[:, None] - B_LEFT_CHUNKS + (jnp.arange(B_BAND) // CHUNK)[None, :]
    mask = (k_chunk >= 0)[None, :, None, None, :]
    p = jax.nn.softmax(jnp.where(mask, s, NEG), axis=-1)
    o = jnp.einsum('bnhqk,bnkhd->bnqhd', p.astype(vb.dtype), vb)
    return o.reshape(Bn, S, B_WIDTH)


def forgetting_mixer(q, k, v, f_logit, f_bias):
    Bn, S, _ = q.shape
    q = q.reshape(Bn, S, C_HEADS, C_HD).transpose(0, 2, 1, 3)
    k = k.reshape(Bn, S, C_HEADS, C_HD).transpose(0, 2, 1, 3)
    v = v.reshape(Bn, S, C_HEADS, C_HD).transpose(0, 2, 1, 3)
    log_f = jax.nn.log_sigmoid(f_logit.astype(jnp.float32) + f_bias.astype(jnp.float32))
    F = jnp.cumsum(log_f, axis=1).transpose(0, 2, 1)
    scale = C_HD ** -0.5
    outs = []
    for i in range(S // Q_BLOCK):
        q0 = i * Q_BLOCK
        kend = q0 + Q_BLOCK
        s = jnp.einsum('bhqd,bhkd->bhqk', q[:, :, q0:kend], k[:, :, :kend],
                       preferred_element_type=jnp.float32) * scale
        s = s + F[:, :, q0:kend, None] - F[:, :, None, :kend]
        mask = jnp.arange(kend)[None, :] <= (q0 + jnp.arange(Q_BLOCK))[:, None]
        outs.append(attend(s, mask, v[:, :, :kend]))
    o = jnp.concatenate(outs, axis=2)
    return o.transpose(0, 2, 1, 3).reshape(Bn, S, C_WIDTH)


def hybrid_layer(x, c_act, cos, sin, w_ada, b_ada, norm_g, w_in, a_q_norm_g, a_w_uq,
                 a_kv_norm_g, a_w_ukv, b_rel_bias, c_forget_b, w_out):
    mod = c_act @ w_ada + b_ada
    shift, scale, gate = jnp.split(mod, 3, axis=-1)
    h = rmsnorm(x, norm_g) * (1.0 + scale[:, None, :]) + shift[:, None, :]
    z = h @ w_in
    (a_cq, a_ckv, a_kpe, a_gate,
     b_q, b_k, b_v, b_gate,
     c_q, c_k, c_v, c_f, c_gate) = split_cols(z, IN_SIZES)
    a_out = mla_mixer(a_cq, a_ckv, a_kpe, a_q_norm_g, a_w_uq, a_kv_norm_g, a_w_ukv, cos, sin)
    b_out = chunk_band_mixer(b_q, b_k, b_v, b_rel_bias)
    c_out = forgetting_mixer(c_q, c_k, c_v, c_f, c_forget_b)
    y = jnp.concatenate([a_out * jax.nn.silu(a_gate),
                         b_out * jax.nn.silu(b_gate),
                         c_out * jax.nn.silu(c_gate)], axis=-1) @ w_out
    return x + gate[:, None, :] * y


def setup_inputs(seed: int = 0) -> dict:
    key = jax.random.key(seed)
    ks = jax.random.split(key, 16)
    f32 = jnp.float32
    x = jax.random.normal(ks[0], (BATCH, SEQ, D_MODEL), f32)
    c = jax.random.normal(ks[1], (BATCH, D_MODEL), f32)
    start = jax.random.randint(ks[2], (BATCH, 1), 0, 64, dtype=jnp.int32) * CHUNK
    positions = (start + jnp.arange(SEQ, dtype=jnp.int32)[None, :]).astype(jnp.int32)
    w_ada = jax.random.normal(ks[3], (DEPTH, D_MODEL, 3 * D_MODEL), f32) * D_MODEL ** -0.5
    b_ada = jax.random.normal(ks[4], (DEPTH, 3 * D_MODEL), f32) * 0.02
    norm_g = 1.0 + 0.02 * jax.random.normal(ks[5], (DEPTH, D_MODEL), f32)
    w_in = jax.random.normal(ks[6], (DEPTH, D_MODEL, N_IN), f32) * D_MODEL ** -0.5
    a_q_norm_g = 1.0 + 0.02 * jax.random.normal(ks[7], (DEPTH, A_Q_RANK), f32)
    a_w_uq = jax.random.normal(ks[8], (DEPTH, A_Q_RANK, A_HEADS * (A_NOPE + A_ROPE)), f32) * A_Q_RANK ** -0.5
    a_kv_norm_g = 1.0 + 0.02 * jax.random.normal(ks[9], (DEPTH, A_KV_RANK), f32)
    a_w_ukv = jax.random.normal(ks[10], (DEPTH, A_KV_RANK, A_HEADS * (A_NOPE + A_V)), f32) * A_KV_RANK ** -0.5
    b_rel_bias = 0.2 * jax.random.normal(ks[11], (DEPTH, B_HEADS, 2 * REL_CLIP + 1), f32)
    c_forget_b = FORGET_BIAS_INIT + 0.1 * jax.random.normal(ks[12], (DEPTH, C_HEADS), f32)
    w_out = jax.random.normal(ks[13], (DEPTH, D_MIX, D_MODEL), f32) * D_MIX ** -0.5
    final_g = 1.0 + 0.02 * jax.random.normal(ks[14], (D_MODEL,), f32)
    return {"x": x, "c": c, "positions": positions, "w_ada": w_ada, "b_ada": b_ada,
            "norm_g": norm_g, "w_in": w_in, "a_q_norm_g": a_q_norm_g, "a_w_uq": a_w_uq,
            "a_kv_norm_g": a_kv_norm_g, "a_w_ukv": a_w_ukv, "b_rel_bias": b_rel_bias,
            "c_forget_b": c_forget_b, "w_out": w_out, "final_g": final_g}


def reference(x, c, positions, w_ada, b_ada, norm_g, w_in, a_q_norm_g, a_w_uq,
              a_kv_norm_g, a_w_ukv, b_rel_bias, c_forget_b, w_out, final_g):
    cos, sin = rope_tables(positions)
    c_act = jax.nn.silu(c)
    for l in range(DEPTH):
        x = hybrid_layer(x, c_act, cos, sin, w_ada[l], b_ada[l], norm_g[l], w_in[l],
                         a_q_norm_g[l], a_w_uq[l], a_kv_norm_g[l], a_w_ukv[l],
                         b_rel_bias[l], c_forget_b[l], w_out[l])
    return rmsnorm(x, final_g)
```

```python
import numpy as np
from contextlib import ExitStack
from functools import partial
import concourse.bass as bass
import concourse.mybir as mybir
from concourse.bass_utils import run_bass_kernel_spmd

F32 = mybir.dt.float32
BF16 = mybir.dt.bfloat16
I32 = mybir.dt.int32
AF = mybir.ActivationFunctionType
ALU = mybir.AluOpType

NCORES = 8
L = 2
T = 2048
D = 1024
NB = T // 128
NT = T // 512
EPS = 1e-6
N_IN = 3621

ENGS = ("pe", "act", "dve", "pool", "sp")
SAMEENG_RAW = ("act", "dve", "pool")


class Op:
    __slots__ = ("eng", "fn", "deps", "sig", "idx", "dma", "semkey", "is_target")

    def __init__(self, eng, fn, dma, semkey):
        self.eng = eng
        self.fn = fn
        self.deps = set()
        self.sig = None
        self.dma = dma
        self.semkey = semkey
        self.is_target = False


class Prog:
    def __init__(self, nc):
        self.nc = nc
        self.ops = []
        self.state = {}
        self.last_dma_on_key = {}

    @staticmethod
    def _conflict(a, ma, b, mb):
        if ma == "r" and mb == "r":
            return False
        if a.dma or b.dma or a.eng != b.eng:
            return True
        if a.eng in SAMEENG_RAW:
            return ma == "w" or mb == "w"
        return False

    def add(self, eng, fn, r=(), w=(), x=(), dma=False, semkey=None):
        op = Op(eng, fn, dma, semkey)
        op.idx = len(self.ops)
        acc = [(k, "r") for k in r] + [(k, "w") for k in w] + [(k, "x") for k in x]
        for k, m in acc:
            st = self.state.get(k)
            if st is None:
                st = [None, []]
                self.state[k] = st
            last, readers = st
            if last is not None and self._conflict(last[0], last[1], op, m):
                op.deps.add(last[0].idx)
            if m == "r":
                readers.append((op, m))
            else:
                for a, ma in readers:
                    if a is not op and self._conflict(a, ma, op, m):
                        op.deps.add(a.idx)
                st[0] = (op, m)
                st[1] = []
        if dma:
            prev = self.last_dma_on_key.get(semkey)
            if prev is not None:
                op.deps.add(prev.idx)
            self.last_dma_on_key[semkey] = op
        op.deps.discard(op.idx)
        self.ops.append(op)
        return op

    def dma(self, q, out, in_, r=(), w=(), semkey=None, **kw):
        return self.add(q, lambda e: e.dma_start(out=out, in_=in_, **kw), r=r, w=w, dma=True, semkey=semkey)

    def emit(self):
        nc = self.nc
        ops = self.ops
        for op in ops:
            best = {}
            for d in op.deps:
                dop = ops[d]
                key = ("dma", dop.semkey) if dop.dma else ("eng", dop.eng)
                if key not in best or d > best[key]:
                    best[key] = d
            op.deps = set(best.values())
        for op in ops:
            for d in op.deps:
                ops[d].is_target = True
        eng_count = {e: 0 for e in ENGS}
        dma_count = {}
        for op in ops:
            if op.dma:
                c = dma_count.get(op.semkey, 0) + 1
                dma_count[op.semkey] = c
                op.sig = 16 * c
            elif op.is_target:
                eng_count[op.eng] += 1
                op.sig = eng_count[op.eng]
        with ExitStack() as es:
            eng_sem = {e: es.enter_context(nc.semaphore("s_" + e)) for e in ENGS}
            dma_sem = {k: es.enter_context(nc.semaphore("d%d" % i)) for i, k in enumerate(dma_count)}
            block = es.enter_context(nc.Block())
            per_eng = {e: [op for op in ops if op.eng == e] for e in ENGS}
            tail = {e: [] for e in ENGS}
            for k, op in self.last_dma_on_key.items():
                tail[op.eng].append(op)

            def semval(op):
                if op.dma:
                    return dma_sem[op.semkey], op.sig
                return eng_sem[op.eng], op.sig

            def run_engine(ename, eng):
                waited = {}
                for op in per_eng[ename]:
                    for d in sorted(op.deps):
                        s, v = semval(ops[d])
                        if waited.get(id(s), 0) >= v:
                            continue
                        waited[id(s)] = v
                        eng.wait_ge(s, v)
                    ins = op.fn(eng)
                    if op.dma:
                        ins.then_inc(dma_sem[op.semkey], 16)
                    elif op.is_target:
                        ins.then_inc(eng_sem[ename], 1)
                for op in tail[ename]:
                    s, v = semval(op)
                    eng.wait_ge(s, v)

            @block.tensor
            def _(e):
                run_engine("pe", e)

            @block.scalar
            def _(e):
                run_engine("act", e)

            @block.vector
            def _(e):
                run_engine("dve", e)

            @block.gpsimd
            def _(e):
                run_engine("pool", e)

            @block.sync
            def _(e):
                run_engine("sp", e)
        self.eng_count = eng_count
        self.n_dma_sems = len(dma_count)


class Rot:
    def __init__(self, items):
        self.items = list(items)
        self.i = 0

    def next(self):
        v = self.items[self.i % len(self.items)]
        self.i += 1
        return v


def _win_blocks():
    blocks = {}
    blocks["cq"] = list(range(0, 384))
    blocks["ckv"] = list(range(384, 640))
    blocks["kpe"] = list(range(576, 672)) + list(range(576, 640)) + list(range(656, 672)) + list(range(640, 656))

    def gate_col(m):
        if m < 384:
            return 672 + m
        if m < 704:
            return 2016 + (m - 384)
        return 3301 + (m - 704)

    for p in range(8):
        blocks["gate%d" % p] = [gate_col(m) for m in range(128 * p, 128 * p + 128)]
    for h in range(5):
        blocks["bqk%d" % h] = list(range(1056 + 64 * h, 1056 + 64 * h + 64)) + list(range(1376 + 64 * h, 1376 + 64 * h + 64))
        blocks["cqk%d" % h] = list(range(2336 + 64 * h, 2336 + 64 * h + 64)) + list(range(2656 + 64 * h, 2656 + 64 * h + 64))
    blocks["bv"] = list(range(1696, 2016))
    blocks["cv"] = list(range(2976, 3296)) + list(range(3296, 3301))
    return blocks


WIN_BLOCKS = _win_blocks()
WIN_OFF = {}
_o = 0
for _k, _v in WIN_BLOCKS.items():
    WIN_OFF[_k] = _o
    _o += 8 * len(_v)
WIN_TOT = _o


def _prep_shared(w_ada, b_ada, norm_g, w_in, a_q_norm_g, a_w_uq, a_kv_norm_g, a_w_ukv, b_rel_bias,
                 c_forget_b, w_out, final_g):
    f = np.float32
    sh = {}
    sh["wada"] = np.ascontiguousarray(np.asarray(w_ada, f).reshape(L, 8, 128, 24, 128).transpose(0, 3, 2, 1, 4))
    sh["bada"] = np.ascontiguousarray(np.broadcast_to(np.asarray(b_ada, f)[:, None, :], (L, 2, 3 * D)))
    sh["ng"] = np.ascontiguousarray(np.asarray(norm_g, f).reshape(L, 8, 128).transpose(0, 2, 1))
    w_in = np.asarray(w_in, f)
    win = np.empty((L, 128, WIN_TOT), f)
    w4 = w_in.reshape(L, 8, 128, N_IN)
    for k, cols in WIN_BLOCKS.items():
        blk = w4[:, :, :, cols]
        win[:, :, WIN_OFF[k]:WIN_OFF[k] + 8 * len(cols)] = blk.transpose(0, 2, 1, 3).reshape(L, 128, 8 * len(cols))
    sh["win"] = win
    sh["gq"] = np.ascontiguousarray(np.asarray(a_q_norm_g, f).reshape(L, 3, 128).transpose(0, 2, 1))
    sh["gkv"] = np.ascontiguousarray(np.asarray(a_kv_norm_g, f).reshape(L, 2, 128).transpose(0, 2, 1))
    uq_cols = []
    for h in range(6):
        b = 96 * h
        uq_cols += list(range(b, b + 96)) + list(range(b, b + 64)) + list(range(b + 80, b + 96)) + list(range(b + 64, b + 80))
    wuq = np.asarray(a_w_uq, f)[:, :, uq_cols]
    sh["wuq"] = np.ascontiguousarray(wuq.reshape(L, 3, 128, 1152).transpose(0, 2, 1, 3))
    ukv_cols = []
    for h in range(6):
        ukv_cols += list(range(128 * h, 128 * h + 64))
    for h in range(6):
        ukv_cols += list(range(128 * h + 64, 128 * h + 128))
    wukv = np.asarray(a_w_ukv, f)[:, :, ukv_cols]
    sh["wukv"] = np.ascontiguousarray(wukv.reshape(L, 2, 128, 768).transpose(0, 2, 1, 3))
    rb = np.asarray(b_rel_bias, f)
    kk, qq = np.meshgrid(np.arange(128), np.arange(256), indexing="ij")
    idx = np.clip(qq - kk, -128, 128) + 128
    sh["rbt"] = np.ascontiguousarray(rb[:, :, idx])
    sh["rbc"] = np.ascontiguousarray(np.broadcast_to(rb[:, None, :, 256], (L, 128, 5)))
    cfb = np.asarray(c_forget_b, f)
    sh["cfb"] = np.ascontiguousarray(np.broadcast_to(np.tile(cfb, (1, NB))[:, None, :], (L, 128, NB * 5)))
    sh["wout"] = np.ascontiguousarray(np.asarray(w_out, f).reshape(L, 8, 128, D).transpose(0, 2, 1, 3))
    sh["fg"] = np.ascontiguousarray(np.broadcast_to(np.asarray(final_g, f)[None, :], (128, D)))
    inv = (np.float32(10000.0) ** (-np.arange(0, 32, 2, dtype=np.float32) / np.float32(32))).astype(f)
    consts = np.zeros((128, 4), f)
    p = np.arange(128)
    consts[:, 0] = inv[p % 16]
    consts[:, 1] = np.where(p % 32 < 16, -1.0, 1.0)
    consts[:, 2] = EPS
    sh["consts"] = consts
    return sh


IN_SHAPES = {
    "x": ([2, T, D], F32), "cT": ([128, 16], F32), "pos": ([2, 128, T], I32),
    "wada": ([L, 24, 128, 8, 128], F32), "bada": ([L, 2, 3 * D], F32), "ng": ([L, 128, 8], F32),
    "win": ([L, 128, WIN_TOT], F32), "gq": ([L, 128, 3], F32), "gkv": ([L, 128, 2], F32),
    "wuq": ([L, 128, 3, 1152], F32), "wukv": ([L, 128, 2, 768], F32),
    "rbt": ([L, 5, 128, 256], F32), "rbc": ([L, 128, 5], F32), "cfb": ([L, 128, NB * 5], F32),
    "wout": ([L, 128, 8, D], F32), "fg": ([128, D], F32), "consts": ([128, 4], F32),
}


def build_program(layers=(0, 1), final=True, nseq=2):
    nc = bass.Bass("TRN2", target_bir_lowering=False)
    dr = {k: nc.dram_tensor(k, s, dt, kind="ExternalInput").ap() for k, (s, dt) in IN_SHAPES.items()}
    out_d = nc.dram_tensor("out", [2, T, D], F32, kind="ExternalOutput").ap()
    x1_d = nc.dram_tensor("x1s", [2, T, D], F32).ap()
    modd = nc.dram_tensor("modd", [L, 2, 3 * D], F32).ap()
    P = Prog(nc)
    es = ExitStack()
    with es:
        def sb(name, shape, dt):
            return es.enter_context(nc.sbuf_tensor("t_" + name, shape, dt))

        ps2 = [es.enter_context(nc.psum_tensor("ps%d" % i, [128, 1024], F32)) for i in range(4)]
        PK = [("ps", i) for i in range(8)]

        def BK(i, rows=slice(None), c0=0, c1=512):
            o = (i % 2) * 512
            return ps2[i // 2][rows, o + c0:o + c1]

        St = Rot([0, 1])
        Ob = Rot([4, 5])
        Jn = Rot([6, 7])
        Jw = Rot([6, 7, 0, 1, 2, 3])
        cur = {"l": 0, "s": 0, "J": Jw}

        def Jnext():
            return cur["J"].next()

        hT = sb("hT", [128, 8, T], BF16)
        GT = sb("GT", [128, 8, T], BF16)
        Vg = sb("Vg", [128, NB, 5, 128], BF16)
        QT0 = sb("QT0", [128, T], BF16)
        KT0 = sb("KT0", [128, T], BF16)
        cqn = sb("cqn", [128, 3, T], BF16)
        ckvn = sb("ckvn", [128, 2, T], BF16)
        CC = sb("CC", [128, T], BF16)
        SS = sb("SS", [128, T], BF16)
        PT = [sb("PT%d" % i, [128, 1024], BF16) for i in range(4)]
        SG0 = sb("SG", [128, T], BF16)
        wbig = sb("wbig", [128, 8192], BF16)
        wuq = sb("wuq", [128, 3, 1152], BF16)
        wukv = sb("wukv", [128, 2, 768], BF16)
        wblk = [sb("wblk%d" % i, [128, 8, 128], BF16) for i in range(4)]
        Eb = sb("Eb", [128, 5, 256], BF16)
        ident = sb("ident", [128, 128], F32)
        ones_bf = sb("ones_bf", [128, 128], BF16)
        cmask = sb("cmask", [128, 128], BF16)
        tri = sb("tri", [128, 128], F32)
        ones32 = sb("ones32", [128, 128], F32)
        consts = sb("consts", [128, 4], F32)
        cT = sb("cTt", [128, 16], F32)
        cact = sb("cact", [128, 16], F32)
        modT = sb("modT", [128, L * 2 * 24], F32)
        amod = sb("amod", [128, L * 2 * 8], F32)
        ngt = sb("ngt", [128, L * 8], F32)
        gqt = sb("gqt", [128, L * 3], F32)
        gkvt = sb("gkvt", [128, L * 2], F32)
        gate_bc = sb("gate_bc", [128, D], F32)
        fg_bc = sb("fg_bc", [128, D], F32)
        rbc = sb("rbc", [128, L * 5], F32)
        nrbc = sb("nrbc", [128, L * 5], F32)
        cfbt = sb("cfbt", [128, NB * 5], F32)
        xt = [sb("xt%d" % i, [128, D], F32) for i in range(2)]
        xn = [sb("xn%d" % i, [128, D], F32) for i in range(1)]
        scr = [sb("scr%d" % i, [128, 512], F32) for i in range(4)]
        scrb = [sb("scrb%d" % i, [128, 512], BF16) for i in range(2)]
        small = sb("small", [128, 64], F32)
        modsb = [sb("modsb%d" % i, [2, 128], F32) for i in range(2)]
        badat = [sb("badat%d" % i, [2, 128], F32) for i in range(2)]
        ZF = sb("ZF", [128, NB * 5], F32)
        Wc = sb("Wc", [128, NB * 5], F32)
        Whi = sb("Whi", [128, NB * 5], BF16)
        Wt32 = sb("Wt32", [128, NB * 5], F32)
        posi = sb("posi", [128, 512], I32)
        kint = posi
        ang, kf = scr[0], scr[1]
        QT = [QT0, xt[1].bitcast(BF16)]
        KT = [KT0, xn[0].bitcast(BF16)]
        QK_ = [("QT", 0), ("xt", 1)]
        KK_ = [("KT", 0), ("xn", 0)]
        SG = [SG0, gate_bc.bitcast(BF16)]
        SGK = ["SG", "gate_bc"]
        FTq, FTk = cqn[:, 0, :], cqn[:, 1, :]
        ckvn32 = ckvn.bitcast(F32)
        AUGq = ckvn32[:, 0, 0:480].rearrange("p (b h r) -> p b h r", b=NB, h=5)
        AUGk = ckvn32[:, 1, 0:480].rearrange("p (b h r) -> p b h r", b=NB, h=5)
        V5 = fg_bc.bitcast(BF16)[:, :].rearrange("p (b d) -> p b d", b=NB)
        KPEr = slice(96, 128)

        def Vslot(slot):
            return Vg[:, :, slot, :] if slot < 5 else V5

        def VK(slot):
            return ("V", slot) if slot < 5 else "fg_bc"

        scr_rot = Rot([0, 1, 2, 3])
        scrb_rot = Rot([0, 1])
        xt_rot = Rot([0, 1])
        pt_rot = Rot([0, 1, 2, 3])
        wada_rot = Rot([0, 1])

        def mm(out, lhsT, rhs, start, stop, r, x, skip=False):
            if skip:
                P.add("pe", lambda e: e.matmul(out, lhsT=lhsT, rhs=rhs, start=start, stop=stop, skip_group_check=True), r=r, x=x)
            else:
                P.add("pe", lambda e: e.matmul(out, lhsT=lhsT, rhs=rhs, start=start, stop=stop), r=r, x=x)

        def dve(fn, r=(), w=(), x=()):
            P.add("dve", fn, r=r, w=w, x=x)

        def act(fn, r=(), w=(), x=()):
            P.add("act", fn, r=r, w=w, x=x)

        def pool(fn, r=(), w=(), x=()):
            P.add("pool", fn, r=r, w=w, x=x)

        def copy(eng, out, in_, r=(), w=(), x=()):
            P.add(eng, lambda e: e.tensor_copy(out=out, in_=in_), r=r, w=w, x=x)

        def tt_op(eng, out, in0, in1, op, r=(), w=(), x=()):
            P.add(eng, lambda e: e.tensor_tensor(out=out, in0=in0, in1=in1, op=op), r=r, w=w, x=x)

        def stt(eng, out, in0, scalar, in1, op0, op1, r=(), w=(), x=()):
            P.add(eng, lambda e: e.scalar_tensor_tensor(out=out, in0=in0, scalar=scalar, in1=in1, op0=op0, op1=op1), r=r, w=w, x=x)

        def tss(eng, out, in_, scalar, op, r=(), w=(), x=()):
            P.add(eng, lambda e: e.tensor_single_scalar(out=out, in_=in_, scalar=scalar, op=op), r=r, w=w, x=x)

        def ts2(eng, out, in0, s1, s2, op0, op1, r=(), w=(), x=()):
            P.add(eng, lambda e: e.tensor_scalar(out=out, in0=in0, scalar1=s1, scalar2=s2, op0=op0, op1=op1), r=r, w=w, x=x)

        def actf(out, in_, func, r=(), w=(), x=(), **kw):
            P.add("act", lambda e: e.activation(out=out, in_=in_, func=func, **kw), r=r, w=w, x=x)

        def memset(eng, ap, val, w=()):
            P.add(eng, lambda e: e.memset(ap, val), w=w)

        def recip(out, in_, r=(), w=(), x=()):
            P.add("dve", lambda e: e.reciprocal(out=out, in_=in_), r=r, w=w, x=x)

        P.dma("sp", consts[:], dr["consts"], w=["consts"], semkey="c0")
        P.dma("sp", cT[:], dr["cT"], w=["cT"], semkey="c1")
        P.dma("sp", ngt[:].rearrange("p (l c) -> p l c", l=L), dr["ng"].rearrange("l p c -> p l c"), w=["ngt"], semkey="c2")
        P.dma("sp", gqt[:].rearrange("p (l c) -> p l c", l=L), dr["gq"].rearrange("l p c -> p l c"), w=["gqt"], semkey="c3")
        P.dma("sp", gkvt[:].rearrange("p (l c) -> p l c", l=L), dr["gkv"].rearrange("l p c -> p l c"), w=["gkvt"], semkey="c4")
        P.dma("sp", rbc[:].rearrange("p (l c) -> p l c", l=L), dr["rbc"].rearrange("l p c -> p l c"), w=["rbc"], semkey="c5")
        memset("pool", ident[:], 0.0, w=["ident"])
        pool(lambda e: e.affine_select(out=ident[:], in_=ident[:], pattern=[[-1, 128]], compare_op=ALU.not_equal,
                                       fill=1.0, base=0, channel_multiplier=1), r=["ident"], w=["ident"])
        memset("pool", ones_bf[:], 1.0, w=["ones_bf"])
        memset("pool", ones32[:], 1.0, w=["ones32"])
        memset("pool", tri[:], 1.0, w=["tri"])
        pool(lambda e: e.affine_select(out=tri[:], in_=tri[:], pattern=[[1, 128]], compare_op=ALU.is_ge,
                                       fill=0.0, base=0, channel_multiplier=-1), r=["tri"], w=["tri"])
        for t_ in range(4):
            memset("dve", ps2[t_][:], 0.0, w=[PK[2 * t_], PK[2 * t_ + 1]])
        tss("dve", nrbc[:], rbc[:], -1.0, ALU.mult, r=["rbc"], w=["nrbc"])
        memset("pool", Vg[:], 1.0, w=[("V", h) for h in range(5)])
        memset("pool", cmask[:], 1.0, w=["cmask"])
        pool(lambda e: e.affine_select(out=cmask[:], in_=cmask[:], pattern=[[1, 128]], compare_op=ALU.is_ge,
                                       fill=0.0, base=0, channel_multiplier=-1), r=["cmask"], w=["cmask"])

        def adaln_prologue(hook):
            actf(cact[:], cT[:], AF.Silu, r=["cT"], w=["cact"])
            stg = [(xt[0], ("xt", 0)), (xt[1], ("xt", 1)), (xn[0], ("xn", 0))]
            stg_i = 0
            for l in layers:
                for n in range(24):
                    tile_, tkey = stg[stg_i % 3]
                    wt = stg_i % 2
                    q1, q2 = "sp", "act"
                    stg_i += 1
                    cs = slice(n * 128, (n + 1) * 128)
                    wv_ = tile_[:].rearrange("p (k n) -> p k n", k=8)
                    P.dma(q1, wv_, dr["wada"][l, n], w=[tkey], semkey=tkey)
                    P.dma(q2, badat[wt][:, 0:128], dr["bada"][l][:, cs], w=[("badat", wt)], semkey=("badat", wt))
                    b = Jnext()
                    for kc in range(8):
                        mm(BK(b, slice(0, 2), 0, 128), cact[:, 2 * kc:2 * kc + 2], wv_[:, kc, :], kc == 0, kc == 7,
                           r=["cact", tkey], x=[PK[b]])
                    tt_op("dve", modsb[wt][:, 0:128], BK(b, slice(0, 2), 0, 128), badat[wt][:, 0:128], ALU.add,
                          r=[("badat", wt)], x=[PK[b]], w=[("modsb", wt)])
                    P.dma(q2, modd[l][:, cs], modsb[wt][:, 0:128], r=[("modsb", wt)], w=[("modd", l)], semkey=("moddw", wt))
                    hook(stg_i)
                for s in range(nseq):
                    o = (l * 2 + s) * 24
                    for j3 in range(3):
                        P.dma("act", modT[:, o + 8 * j3:o + 8 * j3 + 8], modd[l, s, j3 * D:(j3 + 1) * D].rearrange("(j p) -> p j", p=128),
                              r=[("modd", l)], w=["modT"], semkey=("modT", j3), allow_slow_non_contiguous=True)
                    oa = (l * 2 + s) * 8
                    stt("dve", amod[:, oa:oa + 8], modT[:, o + 8:o + 16], 1.0, ngt[:, l * 8:(l + 1) * 8], ALU.add, ALU.mult,
                        r=["modT", "ngt"], w=["amod"])


        def load_wblock(name, dst_ap, key):
            n = len(WIN_BLOCKS[name])
            src = dr["win"][cur["l"]][:, WIN_OFF[name]:WIN_OFF[name] + 8 * n].rearrange("p (k n) -> p k n", k=8)
            P.dma("pool", dst_ap, src, w=[key], semkey=key)

        def proj(wfn, M, nk, wkeys, rhs_fn, rkeys, ncols=512):
            b = Jnext()
            for kc in range(nk):
                mm(BK(b, slice(0, M), 0, ncols), wfn(kc), rhs_fn(kc), kc == 0, kc == nk - 1, r=list(wkeys) + list(rkeys), x=[PK[b]])
            return b

        def hrhs(tt):
            return lambda kc: hT[:, kc, tt * 512:(tt + 1) * 512]

        CH = 4

        def proj_ms(wfn, M, nk, wkeys, rhs_fn, rkeys, after, ncols=512, chunk=None):
            chunk = chunk or CH
            hold = {}

            def part(k0, k1, lastp):
                if k0 == 0:
                    hold["b"] = Jnext()
                b = hold["b"]
                for kc in range(k0, k1):
                    mm(BK(b, slice(0, M), 0, ncols), wfn(kc), rhs_fn(kc), kc == 0, kc == nk - 1, r=list(wkeys) + list(rkeys), x=[PK[b]])
                if lastp:
                    after(b)
            return [partial(part, k0, min(k0 + chunk, nk), k0 + chunk >= nk) for k0 in range(0, nk, chunk)]

        def rope_tables(s):
            for tt in range(NT):
                rope_tile(s, tt)

        def rope_tile(s, tt):
            C1 = 6.28125
            C2 = float(2 * np.pi - 6.28125)
            if True:
                ts = slice(tt * 512, (tt + 1) * 512)
                P.dma("sp", posi[:], dr["pos"][s][:, ts], w=["posi"], semkey="posi")
                copy("dve", ang[:], posi[:], r=["posi"], w=[("scr", 0)])
                tss("dve", ang[:], ang[:], consts[:, 0:1], ALU.mult, r=[("scr", 0), "consts"], w=[("scr", 0)])
                for which, shift, dst in (("sin", 0.0, SS), ("cos", float(np.pi / 2), CC)):
                    tk = "tab" + which
                    s2i = scr_rot.items[2 + (0 if which == "sin" else 1)]
                    tmp = scr[s2i]
                    tkk = ("scr", s2i)
                    ts2("dve", kf[:], ang[:], shift, float(1.0 / (2 * np.pi)), ALU.add, ALU.mult, r=[("scr", 0)], w=[("scr", 1)])
                    copy("dve", kint[:], kf[:], r=[("scr", 1)], w=["posi"])
                    copy("dve", kf[:], kint[:], r=["posi"], w=[("scr", 1)])
                    stt("dve", tmp[:], kf[:], -C1, ang[:], ALU.mult, ALU.add, r=[("scr", 1), ("scr", 0)], w=[tkk])
                    stt("dve", tmp[:], kf[:], -C2, tmp[:], ALU.mult, ALU.add, r=[("scr", 1), tkk], w=[tkk])
                    if shift != 0.0:
                        tss("dve", tmp[:], tmp[:], shift, ALU.add, r=[tkk], w=[tkk])
                    ts2("dve", tmp[:], tmp[:], 3.14159, -3.14159, ALU.min, ALU.max, r=[tkk], w=[tkk])
                    actf(tmp[:], tmp[:], AF.Sin, r=[tkk], w=[tkk])
                    if which == "sin":
                        tss("dve", SS[64:96, ts], tmp[64:96, :], consts[64:96, 1:2], ALU.mult, r=[tkk, "consts"], w=[tk])
                    else:
                        copy("dve", CC[64:96, ts], tmp[64:96, :], r=[tkk], w=[tk])

        def norm_phase(l, s, xsrc, mid_hook=None):
            oa = (l * 2 + s) * 8
            ob = (l * 2 + s) * 24
            memset("dve", small[:, 0:16], 0.0, w=[("sm", k) for k in range(16)])
            for i in range(NB):
                if i == 3 and mid_hook is not None:
                    mid_hook()
                xi = xt_rot.next()
                P.dma("sp", xt[xi][:], xsrc[s, i * 128:(i + 1) * 128, :], r=[("x1", s, i)], w=[("xt", xi)], semkey=("xt", xi))
                junk = GT[:, 0, 0:1024]
                actf(junk, xt[xi][:], AF.Square, r=[("xt", xi)], w=["GT", ("sm", i)], accum_out=small[:, i:i + 1])
                actf(small[:, 16 + i:17 + i], small[:, i:i + 1], AF.Ln, r=[("sm", i), "consts"], w=[("sm2", i)],
                     bias=consts[:, 2:3], scale=1.0 / D)
                actf(small[:, 32 + i:33 + i], small[:, 16 + i:17 + i], AF.Exp, r=[("sm2", i)], w=[("sm3", i)], scale=-0.5)
                actf(xt[xi][:], xt[xi][:], AF.Copy, r=[("xt", xi), ("sm3", i)], w=[("xt", xi)], scale=small[:, 32 + i:33 + i])
                for half in range(2):
                    b = Jnext()
                    for c4 in range(4):
                        c = half * 4 + c4
                        P.add("pe", lambda e, b=b, c4=c4, c=c, xi=xi: e.transpose(out=BK(b, slice(None), c4 * 128, (c4 + 1) * 128),
                                                                                   in_=xt[xi][:, c * 128:(c + 1) * 128], identity=ident[:]),
                              r=[("xt", xi), "ident"], x=[PK[b]])
                    for c4 in range(4):
                        c = half * 4 + c4
                        ts2("dve", hT[:, c, i * 128:(i + 1) * 128], BK(b, slice(None), c4 * 128, (c4 + 1) * 128),
                            amod[:, oa + c:oa + c + 1], modT[:, ob + c:ob + c + 1], ALU.mult, ALU.add,
                            r=["amod", "modT"], x=[PK[b]], w=["hT"])

        def gate_steps(p):
            wi = 2 + (p % 2)
            sgi = p % 2

            def after(tt, b):
                si = scr_rot.next()
                actf(scr[si][:], BK(b), AF.Exp, x=[PK[b]], w=[("scr", si)], scale=-1.0)
                actf(scr[si][:], scr[si][:], AF.Ln, r=[("scr", si)], w=[("scr", si)], bias=1.0, scale=1.0)
                actf(scr[si][:], scr[si][:], AF.Exp, r=[("scr", si)], w=[("scr", si)], scale=-1.0)
                tt_op("dve", SG[sgi][:, tt * 512:(tt + 1) * 512], BK(b), scr[si][:], ALU.mult,
                      r=[("scr", si)], x=[PK[b]], w=[SGK[sgi]])
            out = []
            for tt in range(NT):
                out += proj_ms(lambda kc: wblk[wi][:, kc, :], 128, 8, [("wblk", wi)], hrhs(tt), ["hT"], partial(after, tt))
            return out

        def attn_steps(g, kind, qi, Krows, scale, vslot, bias_ap=None, bias_keys=(), hB=0):
            pair, par = g // 2, g % 2
            sgt, sgkey = SG[pair % 2], SGK[pair % 2]
            vo = 64 * par
            so = 64 - vo
            Q, K = QT[qi], KT[qi]
            qkey, kkey = QK_[qi], KK_[qi]
            Vs, vkey = Vslot(vslot), VK(vslot)
            groups = []
            for qt in range(NT):
                lst = []
                if kind in ("A", "C"):
                    for j in range(4 * qt + 4):
                        if j < 4 * qt:
                            lst.append((j, 512 * qt, 512 * qt + 512, False))
                        else:
                            lst.append((j, 128 * j, 512 * qt + 512, True))
                else:
                    for m in range(max(0, 4 * qt - 4), 4 * qt + 4):
                        qa = max(128 * m, 512 * qt)
                        qb = min(128 * m + 640, 512 * qt + 512)
                        if qa < qb:
                            lst.append((m, qa, qb, False))
                full = [(qt, n == 0, n == len(lst) - 1) + pc for n, pc in enumerate(lst)]
                for i in range(0, len(full), 2):
                    groups.append(full[i:i + 2])
            st = {}
            obank = {}
            ptof = {}
            fin_pending = []

            def issue_S(gi):
                t = St.next()
                for pi_, (qt, first, last, j, qa, qb, diag) in enumerate(groups[gi]):
                    b = 2 * t + pi_
                    w = qb - qa
                    mm(BK(b, slice(None), 0, w), K[0:Krows, 128 * j:128 * j + 128], Q[0:Krows, qa:qb], True, True,
                       r=[qkey, kkey], x=[PK[b]])
                st[gi] = t

            def exp_fix(gi):
                grp = groups[gi]
                t = st.pop(gi)
                pi = pt_rot.next()
                ptof[gi] = pi
                pk = ("PT", pi)
                ws = [qb - qa for (_, _, _, _, qa, qb, _) in grp]
                kw = dict(scale=scale)
                if bias_ap is not None:
                    kw["bias"] = bias_ap
                xk = [PK[2 * t + i] for i in range(len(grp))]
                if len(grp) == 2:
                    W = max(ws)
                    actf(PT[pi][:].rearrange("p (b c) -> p b c", b=2)[:, :, 0:W],
                         ps2[t][:].rearrange("p (b c) -> p b c", b=2)[:, :, 0:W], AF.Exp, r=list(bias_keys), x=xk, w=[pk], **kw)
                else:
                    actf(PT[pi][:, 0:ws[0]], ps2[t][:, 0:ws[0]], AF.Exp, r=list(bias_keys), x=xk, w=[pk], **kw)
                for pi_, (qt, first, last, j, qa, qb, diag) in enumerate(grp):
                    o = 512 * pi_
                    if kind == "A" and diag:
                        memset("pool", PT[pi][64:128, o:o + 64], 0.0, w=[pk])
                    elif kind == "C" and diag:
                        tt_op("pool", PT[pi][:, o:o + 128], PT[pi][:, o:o + 128], cmask[:], ALU.mult, r=[pk, "cmask"], w=[pk])
                    elif kind == "B":
                        la, lb = qa - 128 * j, qb - 128 * j
                        if la < 256:
                            hi = min(lb, 256)
                            tt_op("pool", PT[pi][:, o:o + hi - la], PT[pi][:, o:o + hi - la], Eb[:, hB, la:hi], ALU.mult,
                                  r=[pk, "Eb"], w=[pk])
                        if lb > 576:
                            lo = max(la, 576)
                            memset("pool", PT[pi][0:64, o + lo - la:o + lb - la], 0.0, w=[pk])

            def do_PV(gi):
                grp = groups[gi]
                pi = ptof.pop(gi)
                pk = ("PT", pi)
                for pi_, (qt, first, last, j, qa, qb, diag) in enumerate(grp):
                    o = 512 * pi_
                    w = qb - qa
                    if first:
                        obank[qt] = Ob.next()
                    ob = obank[qt]
                    c0 = qa - 512 * qt
                    mm(BK(ob, slice(None), c0, c0 + w), Vs[:, j, :], PT[pi][:, o:o + w], first, last, r=[vkey, pk], x=[PK[ob]],
                       skip=True)
                    if last:
                        fin_pending.append((qt, ob))

            def flush_fin():
                while fin_pending:
                    qt, ob = fin_pending.pop(0)
                    s1 = scr_rot.next()
                    s2 = scr_rot.next()
                    cs = slice(qt * 512, (qt + 1) * 512)
                    actf(scr[s1][vo:vo + 64, :], BK(ob, slice(so, so + 64)), AF.Ln, x=[PK[ob]], w=[("scr", s1)])
                    actf(scr[s1][vo:vo + 64, :], scr[s1][vo:vo + 64, :], AF.Exp, r=[("scr", s1)], w=[("scr", s1)], scale=-1.0)
                    tt_op("dve", scr[s2][vo:vo + 64, :], BK(ob, slice(vo, vo + 64)), scr[s1][vo:vo + 64, :], ALU.mult,
                          r=[("scr", s1)], x=[PK[ob]], w=[("scr", s2)])
                    tt_op("dve", GT[vo:vo + 64, pair, cs], scr[s2][vo:vo + 64, :], sgt[vo:vo + 64, cs], ALU.mult,
                          r=[("scr", s2), sgkey], w=["GT"])

            n = len(groups)

            pre_done = {}

            def pre():
                if not pre_done:
                    pre_done[0] = True
                    issue_S(0)

            def step(gi):
                if gi == 0:
                    pre()
                exp_fix(gi)
                if gi >= 1:
                    while deferred:
                        deferred.pop(0)()
                flush_fin()
                if gi + 1 < n:
                    issue_S(gi + 1)
                if gi >= 2:
                    do_PV(gi - 2)

            def tail():
                if n >= 2:
                    do_PV(n - 2)
                do_PV(n - 1)
                deferred.append(flush_fin)
            return [partial(step, gi) for gi in range(n)] + [tail], pre

        def A_loads(l):
            load_wblock("cq", wbig[:, 0:3072].rearrange("p (k n) -> p k n", k=8), "wbig")
            load_wblock("ckv", wbig[:, 3072:5120].rearrange("p (k n) -> p k n", k=8), "wbig")
            load_wblock("kpe", wbig[:, 5120:6656].rearrange("p (k n) -> p k n", k=8), "wbig")
            P.dma("pool", wuq[:], dr["wuq"][l], w=["wuq"], semkey="wuq")
            P.dma("pool", wukv[:], dr["wukv"][l], w=["wukv"], semkey="wukv")

        def A_common(l):
            wcq = wbig[:, 0:3072].rearrange("p (k n) -> p k n", k=8)
            wckv = wbig[:, 3072:5120].rearrange("p (k n) -> p k n", k=8)
            wkpe = wbig[:, 5120:6656].rearrange("p (k n) -> p k n", k=8)
            for tt in range(NT):
                ts = slice(tt * 512, (tt + 1) * 512)
                for (wt, ntile, dstn, dkey, gcol, gkey, nfeat) in ((wcq, 3, cqn, "cqn", gqt, "gqt", 384), (wckv, 2, ckvn, "ckvn", gkvt, "gkvt", 256)):
                    srcs = []
                    for i in range(ntile):
                        b = proj(lambda kc, wt=wt, i=i: wt[:, kc, i * 128:(i + 1) * 128], 128, 8, ["wbig"], hrhs(tt), ["hT"])
                        if i < 2:
                            si = scr_rot.next()
                            tgt, tk = scr[si], ("scr", si)
                        else:
                            tgt, tk = xt[0], ("xt", 0)
                        copy("dve", tgt[:, 0:512], BK(b), x=[PK[b]], w=[tk])
                        srcs.append((tgt, tk))
                    bs = Jnext()
                    for i, (tgt, tk) in enumerate(srcs):
                        bi = scrb_rot.next()
                        actf(scrb[bi][:], tgt[:, 0:512], AF.Square, r=[tk], w=[("scrb", bi)])
                        mm(BK(bs), ones_bf[:], scrb[bi][:], i == 0, i == ntile - 1, r=["ones_bf", ("scrb", bi)], x=[PK[bs]])
                    actf(xn[0][:, 0:512], BK(bs), AF.Ln, r=["consts"], x=[PK[bs]], w=[("xn", 0)], bias=consts[:, 2:3], scale=1.0 / nfeat)
                    actf(xn[0][:, 0:512], xn[0][:, 0:512], AF.Exp, r=[("xn", 0)], w=[("xn", 0)], scale=-0.5)
                    for i, (tgt, tk) in enumerate(srcs):
                        stt("dve", dstn[:, i, ts], tgt[:, 0:512], gcol[:, l * ntile + i:l * ntile + i + 1], xn[0][:, 0:512], ALU.mult, ALU.mult,
                            r=[tk, gkey, ("xn", 0)], w=[dkey])
                b1 = proj(lambda kc: wkpe[:, kc, 0:96], 96, 8, ["wbig"], hrhs(tt), ["hT"])
                b2 = proj(lambda kc: wkpe[:, kc, 96:192], 96, 8, ["wbig"], hrhs(tt), ["hT"])
                s1 = scr_rot.next()
                s2 = scr_rot.next()
                tt_op("dve", scr[s1][64:96, :], BK(b1, slice(64, 96)), CC[64:96, ts], ALU.mult, r=["tabcos"], x=[PK[b1]], w=[("scr", s1)])
                tt_op("dve", scr[s2][64:96, :], BK(b2, slice(64, 96)), SS[64:96, ts], ALU.mult, r=["tabsin"], x=[PK[b2]], w=[("scr", s2)])
                tt_op("dve", CC[KPEr, ts], scr[s1][64:96, :], scr[s2][64:96, :], ALU.add, r=[("scr", s1), ("scr", s2)], w=["KPE"])

        def A_head_steps(l, h, qi):
            g = h
            steps = []
            vslot = h
            Vs, vkey = Vslot(vslot), VK(vslot)

            def q_ms(tt):
                ts = slice(tt * 512, (tt + 1) * 512)
                rf = lambda kc: cqn[:, kc, ts]
                hold = {}

                def after1(b1):
                    hold["b1"] = b1

                def after2(b2):
                    b1 = hold["b1"]
                    copy("dve", QT[qi][0:64, ts], BK(b1, slice(0, 64)), x=[PK[b1]], w=[QK_[qi]])
                    s1 = scr_rot.next()
                    s2 = scr_rot.next()
                    tt_op("dve", scr[s1][64:96, :], BK(b1, slice(64, 96)), CC[64:96, ts], ALU.mult, r=["tabcos"], x=[PK[b1]], w=[("scr", s1)])
                    tt_op("dve", scr[s2][64:96, :], BK(b2, slice(64, 96)), SS[64:96, ts], ALU.mult, r=["tabsin"], x=[PK[b2]], w=[("scr", s2)])
                    tt_op("pool", QT[qi][64:96, ts], scr[s1][64:96, :], scr[s2][64:96, :], ALU.add, r=[("scr", s1), ("scr", s2)], w=[QK_[qi]])
                return (proj_ms(lambda kc: wuq[:, kc, 192 * h:192 * h + 96], 96, 3, ["wuq"], rf, ["cqn"], after1, chunk=3)
                        + proj_ms(lambda kc: wuq[:, kc, 192 * h + 96:192 * h + 192], 96, 3, ["wuq"], rf, ["cqn"], after2, chunk=3))

            def k_ms(tt):
                ts = slice(tt * 512, (tt + 1) * 512)

                def after(b3):
                    copy("dve", KT[qi][0:64, ts], BK(b3, slice(0, 64)), x=[PK[b3]], w=[KK_[qi]])
                    copy("dve", KT[qi][64:96, ts], CC[KPEr, ts], r=["KPE"], w=[KK_[qi]])
                return proj_ms(lambda kc: wukv[:, kc, 64 * h:64 * h + 64], 64, 2, ["wukv"], lambda kc: ckvn[:, kc, ts], ["ckvn"], after)

            def v_ms(half):
                hold = {}

                def part(q):
                    if q == 0:
                        hold["b"] = Jnext()
                    b = hold["b"]
                    for bb in range(8):
                        blk = half * 8 + bb
                        for kc in range(2):
                            mm(BK(b, slice(None), bb * 64, (bb + 1) * 64), ckvn[:, kc, blk * 128:(blk + 1) * 128],
                               wukv[:, kc, 384 + 64 * h:384 + 64 * h + 64], kc == 0, kc == 1, r=["ckvn", "wukv"], x=[PK[b]], skip=True)
                    if True:
                        vo_ = 64 * (g % 2)
                        copy("dve", Vs[:, half * 8:half * 8 + 8, vo_:vo_ + 64], BK(b).rearrange("p (b d) -> p b d", b=8), x=[PK[b]], w=[vkey])
                return [partial(part, 0)]

            for tt in range(NT):
                steps += q_ms(tt)
                steps += k_ms(tt)
            steps += v_ms(0)
            steps += v_ms(1)
            return steps

        def slotBC(kind, h):
            return h if kind == "B" else (h + 5) % 6

        def v_batch_steps(kind, blockname, ncols, with_f, g0):
            wv = wbig[:, 0:8 * ncols].rearrange("p (k n) -> p k n", k=8)

            def after(blk, b):
                for h in range(5):
                    vo_ = 64 * ((g0 + h) % 2)
                    sl = slotBC(kind, h)
                    copy("dve", Vslot(sl)[:, blk, vo_:vo_ + 64], BK(b, slice(None), 64 * h, 64 * h + 64), x=[PK[b]], w=[VK(sl)])
                if with_f:
                    copy("dve", ZF[:, blk * 5:blk * 5 + 5], BK(b, slice(None), 320, 325), x=[PK[b]], w=["ZF"])
            out = []
            for blk in range(NB):
                out += proj_ms(lambda kc, blk=blk: hT[:, kc, blk * 128:(blk + 1) * 128], 128, 8, ["hT"], lambda kc: wv[:, kc, :], ["wbig"],
                               partial(after, blk), ncols=ncols)
            return out

        def qk_steps(g, qi):
            wi = g % 2

            def after(tt, b):
                ts = slice(tt * 512, (tt + 1) * 512)
                copy("dve", QT[qi][0:64, ts], BK(b, slice(0, 64)), x=[PK[b]], w=[QK_[qi]])
                copy("dve", KT[qi][0:64, ts], BK(b, slice(64, 128)), x=[PK[b]], w=[KK_[qi]])
            out = []
            for tt in range(NT):
                out += proj_ms(lambda kc: wblk[wi][:, kc, :], 128, 8, [("wblk", wi)], hrhs(tt), ["hT"], partial(after, tt))
            return out

        def B_common_steps(l):
            def eb():
                for h in range(5):
                    si = scr_rot.next()
                    P.dma("sp", scr[si][:, 0:256], dr["rbt"][l, h], w=[("scr", si)], semkey=("rbt", si))
                    actf(Eb[:, h, :], scr[si][:, 0:256], AF.Exp, r=[("scr", si), "nrbc"], w=["Eb"], bias=nrbc[:, l * 5 + h:l * 5 + h + 1], scale=1.0)
                memset("pool", Eb[64:128, :, 0:64], 0.0, w=["Eb"])
            return [eb] + v_batch_steps("B", "bv", 320, False, 6)

        def C_common_steps(l):
            def fprep():
                tt_op("dve", ZF[:], ZF[:], cfbt[:], ALU.add, r=["ZF", "cfbt"], w=["ZF"])
                actf(ZF[:], ZF[:], AF.Exp, r=["ZF"], w=["ZF"], scale=-1.0)
                actf(ZF[:], ZF[:], AF.Ln, r=["ZF"], w=["ZF"], bias=1.0, scale=1.0)
                b1 = Jnext()
                mm(BK(b1, slice(None), 0, 80), tri[:], ZF[:], True, True, r=["tri", "ZF"], x=[PK[b1]])
                b2 = Jnext()
                mm(BK(b2, slice(None), 0, 80), ones32[:], ZF[:], True, True, r=["ones32", "ZF"], x=[PK[b2]])
                copy("dve", Wt32[:], BK(b2, slice(None), 0, 80), x=[PK[b2]], w=["Wt32"])
                tot = Wt32[:].rearrange("p (b h) -> p h b", h=5)
                pre = Wc[:].rearrange("p (b h) -> p h b", h=5)
                memset("dve", Wc[:], 0.0, w=["Wc"])
                for h in range(5):
                    dve(lambda e, h=h: e.tensor_tensor_scan(out=pre[:, h, 1:NB], data0=ones32[:, 0:NB - 1], data1=tot[:, h, 0:NB - 1],
                                                            initial=0.0, op0=ALU.mult, op1=ALU.add), r=["Wt32", "ones32"], w=["Wc"])
                tt_op("dve", Wc[:], BK(b1, slice(None), 0, 80), Wc[:], ALU.add, r=["Wc"], x=[PK[b1]], w=["Wc"])
                tss("dve", Wc[:], Wc[:], 8.0, ALU.mult, r=["Wc"], w=["Wc"])
                memset("dve", AUGq, 1.0, w=["ckvn"])
                memset("dve", AUGk, 1.0, w=["ckvn"])
                Wv = Wc[:].rearrange("p (b h) -> p b h", h=5)
                Hv = Whi[:].rearrange("p (b h) -> p b h", h=5)
                for r_ in range(3):
                    copy("dve", Whi[:], Wc[:], r=["Wc"], w=["Whi"])
                    copy("dve", AUGk[:, :, :, 3 + r_], Hv, r=["Whi"], w=["ckvn"])
                    tss("dve", AUGq[:, :, :, r_], Hv, -1.0, ALU.mult, r=["Whi"], w=["ckvn"])
                    if r_ < 2:
                        tt_op("dve", Wv, Wv, AUGk[:, :, :, 3 + r_], ALU.subtract, r=["Wc", "ckvn"], w=["Wc"])
                for (AUG, FT) in ((AUGq, FTq), (AUGk, FTk)):
                    for q4 in range(4):
                        b = Jnext()
                        for bb in range(4):
                            blk = q4 * 4 + bb
                            P.add("pe", lambda e, b=b, bb=bb, blk=blk, AUG=AUG: e.transpose(
                                out=BK(b, slice(0, 30), bb * 128, (bb + 1) * 128), in_=AUG[:, blk, :, :].rearrange("p h r -> p (h r)"),
                                identity=ident[:]), r=["ckvn", "ident"], x=[PK[b]])
                        copy("dve", FT[0:30, q4 * 512:(q4 + 1) * 512], BK(b, slice(0, 30)), x=[PK[b]], w=["cqn"])
            return v_batch_steps("C", "cv", 325, True, 11) + [fprep]

        def C_rows_step(h, qi):
            def f():
                P.dma("sp", QT[qi][64:70, :], FTq[6 * h:6 * h + 6, :], r=["cqn"], w=[QK_[qi]], semkey=("ftq", qi))
                P.dma("sp", KT[qi][64:70, :], FTk[6 * h:6 * h + 6, :], r=["cqn"], w=[KK_[qi]], semkey=("ftk", qi))
            return f

        def head_kind(g):
            return "A" if g < 6 else ("B" if g < 11 else "C")

        def prefetch_for(l, g):
            def f():
                if g >= 16:
                    return
                k = head_kind(g)
                if g % 2 == 0:
                    load_wblock("gate%d" % (g // 2), wblk[2 + (g // 2) % 2][:], ("wblk", 2 + (g // 2) % 2))
                if k == "B":
                    load_wblock("bqk%d" % (g - 6), wblk[g % 2][:], ("wblk", g % 2))
                    if g == 6:
                        load_wblock("bv", wbig[:, 0:8 * 320].rearrange("p (k n) -> p k n", k=8), "wbig")
                elif k == "C":
                    load_wblock("cqk%d" % (g - 11), wblk[g % 2][:], ("wblk", g % 2))
                    if g == 11:
                        load_wblock("cv", wbig[:, 0:8 * 325].rearrange("p (k n) -> p k n", k=8), "wbig")
                        P.dma("sp", cfbt[:], dr["cfb"][l], w=["cfbt"], semkey="cfbt")
                if g == 13:
                    P.dma("pool", wbig[:].rearrange("p (c n) -> p c n", c=8), dr["wout"][l], w=["wbig"], semkey="wbig")
            return f

        def prep_steps(l, g):
            qi = g % 2
            k = head_kind(g)
            steps = []
            if k == "B" and g == 6:
                steps += B_common_steps(l)
            if k == "C" and g == 11:
                steps += C_common_steps(l)
            if g % 2 == 0:
                steps += gate_steps(g // 2)
            if k == "A":
                steps += A_head_steps(l, g, qi)
            else:
                steps += qk_steps(g, qi)
                if k == "C":
                    steps.append(C_rows_step(g - 11, qi))
            steps.insert(len(steps) // 2, prefetch_for(l, g + 1))
            return steps

        def attn_for(l, g):
            qi = g % 2
            k = head_kind(g)
            if k == "A":
                return attn_steps(g, "A", qi, 96, float(96 ** -0.5), g)
            if k == "B":
                h = g - 6
                return attn_steps(g, "B", qi, 64, 0.125, slotBC("B", h), bias_ap=rbc[:, l * 5 + h:l * 5 + h + 1], bias_keys=("rbc",), hB=h)
            h = g - 11
            return attn_steps(g, "C", qi, 70, 0.125, slotBC("C", h))

        deferred = []

        def interleave(main, side, before_last=None):
            nm, nsd = len(main), len(side)
            nme = max(1, int(nm * 0.7))
            done = 0
            for i, st_ in enumerate(main):
                if i == nm - 1 and before_last is not None:
                    while done < nsd:
                        side[done]()
                        done += 1
                    before_last()
                st_()
                tgt = min(nsd, (nsd * (i + 1)) // nme)
                while done < tgt:
                    side[done]()
                    done += 1
            while done < nsd:
                side[done]()
                done += 1

        def out_phase(l, s, xsrc, xdst, do_final):
            wo = wbig[:].rearrange("p (c n) -> p c n", c=8)
            P.dma("sp", gate_bc[:], modd[l, s:s + 1, 2 * D:3 * D].partition_broadcast(128), r=[("modd", l)], w=["gate_bc"], semkey="gate_bc")
            if do_final:
                P.dma("sp", fg_bc[:], dr["fg"], w=["fg_bc"], semkey="fg_bc")
                memset("dve", small[:, 0:16], 0.0, w=[("sm", k) for k in range(16)])
            obufs = [(xt[0], ("xt", 0)), (xt[1], ("xt", 1)), (xn[0], ("xn", 0))]
            for i in range(NB):
                xb, xk = obufs[i % 3]
                P.dma("sp", xb[:], xsrc[s, i * 128:(i + 1) * 128, :], r=[("x1", s, i)], w=[xk], semkey=xk)
                for half in range(2):
                    b = Jnext()
                    for c in range(8):
                        mm(BK(b), GT[:, c, i * 128:(i + 1) * 128], wo[:, c, half * 512:(half + 1) * 512], c == 0, c == 7,
                           r=["GT", "wbig"], x=[PK[b]])
                    hs = slice(half * 512, (half + 1) * 512)
                    si = scr_rot.next()
                    tt_op("dve", scr[si][:], BK(b), gate_bc[:, hs], ALU.mult, r=["gate_bc"], x=[PK[b]], w=[("scr", si)])
                    tt_op("pool", xb[:, hs], scr[si][:], xb[:, hs], ALU.add, r=[("scr", si), xk], w=[xk])
                if do_final:
                    junk = hT[:, 0, 0:1024]
                    actf(junk, xb[:], AF.Square, r=[xk], w=["hT", ("sm", i)], accum_out=small[:, i:i + 1])
                    actf(small[:, 16 + i:17 + i], small[:, i:i + 1], AF.Ln, r=[("sm", i), "consts"], w=[("sm2", i)],
                         bias=consts[:, 2:3], scale=1.0 / D)
                    actf(small[:, 32 + i:33 + i], small[:, 16 + i:17 + i], AF.Exp, r=[("sm2", i)], w=[("sm3", i)], scale=-0.5)
                    stt("dve", xb[:], xb[:], small[:, 32 + i:33 + i], fg_bc[:], ALU.mult, ALU.mult,
                        r=[xk, ("sm3", i), "fg_bc"], w=[xk])
                    P.dma("act", xdst[s, i * 128:(i + 1) * 128, :], xb[:], r=[xk], w=[("outd", s, i)], semkey=("xo", i % 3))
                else:
                    P.dma("act", xdst[s, i * 128:(i + 1) * 128, :], xb[:], r=[xk],
                          w=[("x1", s, i) if xdst is x1_d else ("outd", s, i)], semkey=("xo", i % 3))

        nl = len(layers)

        _rope_done = set()

        def _hook(k):
            if k in (6, 14, 22, 30):
                _rope_done.add((k - 6) // 8)
                rope_tile(0, (k - 6) // 8)
        adaln_prologue(_hook)
        for tt_ in range(NT):
            if tt_ not in _rope_done:
                rope_tile(0, tt_)
        for s in range(nseq):
            cur["J"] = Jw
            memset("pool", V5, 1.0, w=["fg_bc"])
            if s > 0:
                rope_tables(s)
            for li, l in enumerate(layers):
                cur["l"], cur["s"] = l, s
                last = li == nl - 1
                xsrc = dr["x"] if li == 0 else x1_d
                xdst = out_d if last else x1_d
                cur["J"] = Jw
                def _mid(l=l):
                    A_loads(l)
                    prefetch_for(l, 0)()
                norm_phase(l, s, xsrc, mid_hook=_mid)
                A_common(l)
                for st_ in prep_steps(l, 0):
                    st_()
                cur["J"] = Jn
                cur_att = attn_for(l, 0)
                for g in range(16):
                    side = prep_steps(l, g + 1) if g < 15 else []
                    nxt = attn_for(l, g + 1) if g < 15 else None
                    interleave(cur_att[0], side, before_last=(nxt[1] if nxt is not None else None))
                    cur_att = nxt
                while deferred:
                    deferred.pop(0)()
                cur["J"] = Jw
                out_phase(l, s, xsrc, xdst, do_final=(last and final))
        P.sbuf_free = nc.sbuf_bytes_remaining
        P.emit()
    return nc, P


_CACHE = {}


def _get_prog(key):
    if key not in _CACHE:
        _CACHE[key] = build_program(*key)
    return _CACHE[key]


def kernel(x, c, positions, w_ada, b_ada, norm_g, w_in, a_q_norm_g, a_w_uq, a_kv_norm_g, a_w_ukv,
           b_rel_bias, c_forget_b, w_out, final_g):
    sh = _prep_shared(w_ada, b_ada, norm_g, w_in, a_q_norm_g, a_w_uq, a_kv_norm_g, a_w_ukv, b_rel_bias,
                      c_forget_b, w_out, final_g)
    x = np.asarray(x, np.float32)
    c = np.asarray(c, np.float32)
    positions = np.asarray(positions, np.int32)
    in_maps = []
    for i in range(NCORES):
        m = dict(sh)
        m["x"] = np.ascontiguousarray(x[2 * i:2 * i + 2])
        cc = c[2 * i:2 * i + 2]
        m["cT"] = np.ascontiguousarray(cc.reshape(2, 8, 128).transpose(2, 1, 0).reshape(128, 16))
        m["pos"] = np.ascontiguousarray(np.broadcast_to(positions[2 * i:2 * i + 2, None, :], (2, 128, T)))
        in_maps.append(m)
    nc, _ = _get_prog(((0, 1), True, 2))
    res = run_bass_kernel_spmd(nc, in_maps, core_ids=list(range(NCORES)))
    out = np.concatenate([np.asarray(r["out"]) for r in res.results], axis=0)
    return out.astype(np.float32)
```

```python
import numpy as np
from contextlib import ExitStack
from functools import partial
import concourse.bass as bass
import concourse.mybir as mybir
from concourse.bass_utils import run_bass_kernel_spmd

F32 = mybir.dt.float32
BF16 = mybir.dt.bfloat16
I32 = mybir.dt.int32
AF = mybir.ActivationFunctionType
ALU = mybir.AluOpType

NCORES = 8
L = 2
T = 2048
D = 1024
NB = T // 128
NT = T // 512
EPS = 1e-6
N_IN = 3621

ENGS = ("pe", "act", "dve", "pool", "sp")
SAMEENG_RAW = ("act", "dve", "pool")


class Op:
    __slots__ = ("eng", "fn", "deps", "sig", "idx", "dma", "semkey", "is_target")

    def __init__(self, eng, fn, dma, semkey):
        self.eng = eng
        self.fn = fn
        self.deps = set()
        self.sig = None
        self.dma = dma
        self.semkey = semkey
        self.is_target = False


class Prog:
    def __init__(self, nc):
        self.nc = nc
        self.ops = []
        self.state = {}
        self.last_dma_on_key = {}

    @staticmethod
    def _conflict(a, ma, b, mb):
        if ma == "r" and mb == "r":
            return False
        if a.dma or b.dma or a.eng != b.eng:
            return True
        if a.eng in SAMEENG_RAW:
            return ma == "w" or mb == "w"
        return False

    def add(self, eng, fn, r=(), w=(), x=(), dma=False, semkey=None):
        op = Op(eng, fn, dma, semkey)
        op.idx = len(self.ops)
        acc = [(k, "r") for k in r] + [(k, "w") for k in w] + [(k, "x") for k in x]
        for k, m in acc:
            st = self.state.get(k)
            if st is None:
                st = [None, []]
                self.state[k] = st
            last, readers = st
            if last is not None and self._conflict(last[0], last[1], op, m):
                op.deps.add(last[0].idx)
            if m == "r":
                readers.append((op, m))
            else:
                for a, ma in readers:
                    if a is not op and self._conflict(a, ma, op, m):
                        op.deps.add(a.idx)
                st[0] = (op, m)
                st[1] = []
        if dma:
            prev = self.last_dma_on_key.get(semkey)
            if prev is not None:
                op.deps.add(prev.idx)
            self.last_dma_on_key[semkey] = op
        op.deps.discard(op.idx)
        self.ops.append(op)
        return op

    def dma(self, q, out, in_, r=(), w=(), semkey=None, **kw):
        return self.add(q, lambda e: e.dma_start(out=out, in_=in_, **kw), r=r, w=w, dma=True, semkey=semkey)

    def emit(self):
        nc = self.nc
        ops = self.ops
        for op in ops:
            best = {}
            for d in op.deps:
                dop = ops[d]
                key = ("dma", dop.semkey) if dop.dma else ("eng", dop.eng)
                if key not in best or d > best[key]:
                    best[key] = d
            op.deps = set(best.values())
        for op in ops:
            for d in op.deps:
                ops[d].is_target = True
        eng_count = {e: 0 for e in ENGS}
        dma_count = {}
        for op in ops:
            if op.dma:
                c = dma_count.get(op.semkey, 0) + 1
                dma_count[op.semkey] = c
                op.sig = 16 * c
            elif op.is_target:
                eng_count[op.eng] += 1
                op.sig = eng_count[op.eng]
        with ExitStack() as es:
            eng_sem = {e: es.enter_context(nc.semaphore("s_" + e)) for e in ENGS}
            dma_sem = {k: es.enter_context(nc.semaphore("d%d" % i)) for i, k in enumerate(dma_count)}
            block = es.enter_context(nc.Block())
            per_eng = {e: [op for op in ops if op.eng == e] for e in ENGS}
            tail = {e: [] for e in ENGS}
            for k, op in self.last_dma_on_key.items():
                tail[op.eng].append(op)

            def semval(op):
                if op.dma:
                    return dma_sem[op.semkey], op.sig
                return eng_sem[op.eng], op.sig

            def run_engine(ename, eng):
                waited = {}
                for op in per_eng[ename]:
                    for d in sorted(op.deps):
                        s, v = semval(ops[d])
                        if waited.get(id(s), 0) >= v:
                            continue
                        waited[id(s)] = v
                        eng.wait_ge(s, v)
                    ins = op.fn(eng)
                    if op.dma:
                        ins.then_inc(dma_sem[op.semkey], 16)
                    elif op.is_target:
                        ins.then_inc(eng_sem[ename], 1)
                for op in tail[ename]:
                    s, v = semval(op)
                    eng.wait_ge(s, v)

            @block.tensor
            def _(e):
                run_engine("pe", e)

            @block.scalar
            def _(e):
                run_engine("act", e)

            @block.vector
            def _(e):
                run_engine("dve", e)

            @block.gpsimd
            def _(e):
                run_engine("pool", e)

            @block.sync
            def _(e):
                run_engine("sp", e)
        self.eng_count = eng_count
        self.n_dma_sems = len(dma_count)


class Rot:
    def __init__(self, items):
        self.items = list(items)
        self.i = 0

    def next(self):
        v = self.items[self.i % len(self.items)]
        self.i += 1
        return v


def _win_blocks():
    blocks = {}
    blocks["cq"] = list(range(0, 384))
    blocks["ckv"] = list(range(384, 640))
    blocks["kpe"] = list(range(576, 672)) + list(range(576, 640)) + list(range(656, 672)) + list(range(640, 656))

    def gate_col(m):
        if m < 384:
            return 672 + m
        if m < 704:
            return 2016 + (m - 384)
        return 3301 + (m - 704)

    for p in range(8):
        blocks["gate%d" % p] = [gate_col(m) for m in range(128 * p, 128 * p + 128)]
    for h in range(5):
        blocks["bqk%d" % h] = list(range(1056 + 64 * h, 1056 + 64 * h + 64)) + list(range(1376 + 64 * h, 1376 + 64 * h + 64))
        blocks["cqk%d" % h] = list(range(2336 + 64 * h, 2336 + 64 * h + 64)) + list(range(2656 + 64 * h, 2656 + 64 * h + 64))
    blocks["bv"] = list(range(1696, 2016))
    blocks["cv"] = list(range(2976, 3296)) + list(range(3296, 3301))
    return blocks


WIN_BLOCKS = _win_blocks()
WIN_OFF = {}
_o = 0
for _k, _v in WIN_BLOCKS.items():
    WIN_OFF[_k] = _o
    _o += 8 * len(_v)
WIN_TOT = _o


def _prep_shared(w_ada, b_ada, norm_g, w_in, a_q_norm_g, a_w_uq, a_kv_norm_g, a_w_ukv, b_rel_bias,
                 c_forget_b, w_out, final_g):
    f = np.float32
    sh = {}
    sh["wada"] = np.ascontiguousarray(np.asarray(w_ada, f).reshape(L, 8, 128, 24, 128).transpose(0, 3, 2, 1, 4))
    sh["bada"] = np.ascontiguousarray(np.broadcast_to(np.asarray(b_ada, f)[:, None, :], (L, 2, 3 * D)))
    sh["badaT"] = np.ascontiguousarray(np.asarray(b_ada, f).reshape(L, 24, 128).transpose(0, 2, 1))
    sh["ng"] = np.ascontiguousarray(np.asarray(norm_g, f).reshape(L, 8, 128).transpose(0, 2, 1))
    w_in = np.asarray(w_in, f)
    win = np.empty((L, 128, WIN_TOT), f)
    w4 = w_in.reshape(L, 8, 128, N_IN)
    for k, cols in WIN_BLOCKS.items():
        blk = w4[:, :, :, cols]
        win[:, :, WIN_OFF[k]:WIN_OFF[k] + 8 * len(cols)] = blk.transpose(0, 2, 1, 3).reshape(L, 128, 8 * len(cols))
    sh["win"] = win
    sh["gq"] = np.ascontiguousarray(np.asarray(a_q_norm_g, f).reshape(L, 3, 128).transpose(0, 2, 1))
    sh["gkv"] = np.ascontiguousarray(np.asarray(a_kv_norm_g, f).reshape(L, 2, 128).transpose(0, 2, 1))
    uq_cols = []
    for h in range(6):
        b = 96 * h
        uq_cols += list(range(b, b + 96)) + list(range(b, b + 64)) + list(range(b + 80, b + 96)) + list(range(b + 64, b + 80))
    wuq = np.asarray(a_w_uq, f)[:, :, uq_cols]
    sh["wuq"] = np.ascontiguousarray(wuq.reshape(L, 3, 128, 1152).transpose(0, 2, 1, 3))
    ukv_cols = []
    for h in range(6):
        ukv_cols += list(range(128 * h, 128 * h + 64))
    for h in range(6):
        ukv_cols += list(range(128 * h + 64, 128 * h + 128))
    wukv = np.asarray(a_w_ukv, f)[:, :, ukv_cols]
    sh["wukv"] = np.ascontiguousarray(wukv.reshape(L, 2, 128, 768).transpose(0, 2, 1, 3))
    rb = np.asarray(b_rel_bias, f)
    kk, qq = np.meshgrid(np.arange(128), np.arange(256), indexing="ij")
    idx = np.clip(qq - kk, -128, 128) + 128
    sh["rbt"] = np.ascontiguousarray(rb[:, :, idx])
    sh["rbc"] = np.ascontiguousarray(np.broadcast_to(rb[:, None, :, 256], (L, 128, 5)))
    cfb = np.asarray(c_forget_b, f)
    sh["cfb"] = np.ascontiguousarray(np.broadcast_to(np.tile(cfb, (1, NB))[:, None, :], (L, 128, NB * 5)))
    sh["wout"] = np.ascontiguousarray(np.asarray(w_out, f).reshape(L, 8, 128, D).transpose(0, 2, 1, 3))
    sh["fg"] = np.ascontiguousarray(np.broadcast_to(np.asarray(final_g, f)[None, :], (128, D)))
    inv = (np.float32(10000.0) ** (-np.arange(0, 32, 2, dtype=np.float32) / np.float32(32))).astype(f)
    consts = np.zeros((128, 4), f)
    p = np.arange(128)
    consts[:, 0] = inv[p % 16]
    consts[:, 1] = np.where(p % 32 < 16, -1.0, 1.0)
    consts[:, 2] = EPS
    sh["consts"] = consts
    return sh


IN_SHAPES = {
    "x": ([2, T, D], F32), "cT": ([128, 16], F32), "pos": ([2, 128, T], I32),
    "wada": ([L, 24, 128, 8, 128], F32), "bada": ([L, 2, 3 * D], F32), "badaT": ([L, 128, 24], F32), "ng": ([L, 128, 8], F32),
    "win": ([L, 128, WIN_TOT], F32), "gq": ([L, 128, 3], F32), "gkv": ([L, 128, 2], F32),
    "wuq": ([L, 128, 3, 1152], F32), "wukv": ([L, 128, 2, 768], F32),
    "rbt": ([L, 5, 128, 256], F32), "rbc": ([L, 128, 5], F32), "cfb": ([L, 128, NB * 5], F32),
    "wout": ([L, 128, 8, D], F32), "fg": ([128, D], F32), "consts": ([128, 4], F32),
}


def build_program(layers=(0, 1), final=True, nseq=2):
    nc = bass.Bass("TRN2", target_bir_lowering=False)
    dr = {k: nc.dram_tensor(k, s, dt, kind="ExternalInput").ap() for k, (s, dt) in IN_SHAPES.items()}
    out_d = nc.dram_tensor("out", [2, T, D], F32, kind="ExternalOutput").ap()
    x1_d = nc.dram_tensor("x1s", [2, T, D], F32).ap()
    modd = nc.dram_tensor("modd", [L, 2, 3 * D], F32).ap()
    P = Prog(nc)
    es = ExitStack()
    with es:
        def sb(name, shape, dt):
            return es.enter_context(nc.sbuf_tensor("t_" + name, shape, dt))

        ps2 = [es.enter_context(nc.psum_tensor("ps%d" % i, [128, 1024], F32)) for i in range(4)]
        PK = [("ps", i) for i in range(8)]

        def BK(i, rows=slice(None), c0=0, c1=512):
            o = (i % 2) * 512
            return ps2[i // 2][rows, o + c0:o + c1]

        St = Rot([0, 1])
        Ob = Rot([4, 5])
        Jn = Rot([6, 7])
        Jw = Rot([6, 7, 0, 1, 2, 3])
        cur = {"l": 0, "s": 0, "J": Jw}

        def Jnext():
            return cur["J"].next()

        hT = sb("hT", [128, 8, T], BF16)
        GT = sb("GT", [128, 8, T], BF16)
        Vg = sb("Vg", [128, NB, 5, 128], BF16)
        QT0 = sb("QT0", [128, T], BF16)
        KT0 = sb("KT0", [128, T], BF16)
        cqn = sb("cqn", [128, 3, T], BF16)
        ckvn = sb("ckvn", [128, 2, T], BF16)
        CC = sb("CC", [128, T], BF16)
        SS = sb("SS", [128, T], BF16)
        PT = [sb("PT%d" % i, [128, 1024], BF16) for i in range(4)]
        SG0 = sb("SG", [128, T], BF16)
        wbig = sb("wbig", [128, 8192], BF16)
        wuq = sb("wuq", [128, 3, 1152], BF16)
        wukv = sb("wukv", [128, 2, 768], BF16)
        wblk = [sb("wblk%d" % i, [128, 8, 128], BF16) for i in range(4)]
        Eb = sb("Eb", [128, 5, 256], BF16)
        ident = sb("ident", [128, 128], F32)
        ones_bf = sb("ones_bf", [128, 128], BF16)
        cmask = sb("cmask", [128, 128], BF16)
        tri = sb("tri", [128, 128], F32)
        ones32 = sb("ones32", [128, 128], F32)
        consts = sb("consts", [128, 4], F32)
        cT = sb("cTt", [128, 16], F32)
        cact = sb("cact", [128, 16], F32)
        modT = sb("modT", [128, L * 2 * 24], F32)
        badaT = sb("badaT", [128, L * 24], F32)
        amod = sb("amod", [128, L * 2 * 8], F32)
        ngt = sb("ngt", [128, L * 8], F32)
        gqt = sb("gqt", [128, L * 3], F32)
        gkvt = sb("gkvt", [128, L * 2], F32)
        gate_bc = sb("gate_bc", [128, D], F32)
        fg_bc = sb("fg_bc", [128, D], F32)
        rbc = sb("rbc", [128, L * 5], F32)
        nrbc = sb("nrbc", [128, L * 5], F32)
        cfbt = sb("cfbt", [128, NB * 5], F32)
        xt = [sb("xt%d" % i, [128, D], F32) for i in range(2)]
        xn = [sb("xn%d" % i, [128, D], F32) for i in range(1)]
        scr = [sb("scr%d" % i, [128, 512], F32) for i in range(4)]
        scrb = [sb("scrb%d" % i, [128, 512], BF16) for i in range(2)]
        small = sb("small", [128, 64], F32)
        modsb = [sb("modsb%d" % i, [2, 128], F32) for i in range(2)]
        badat = [sb("badat%d" % i, [2, 128], F32) for i in range(2)]
        ZF = sb("ZF", [128, NB * 5], F32)
        Wc = sb("Wc", [128, NB * 5], F32)
        Whi = sb("Whi", [128, NB * 5], BF16)
        Wt32 = sb("Wt32", [128, NB * 5], F32)
        posi = sb("posi", [128, 512], I32)
        kint = posi
        ang, kf = scr[0], scr[1]
        QT = [QT0, xt[1].bitcast(BF16)]
        KT = [KT0, xn[0].bitcast(BF16)]
        QK_ = [("QT", 0), ("xt", 1)]
        KK_ = [("KT", 0), ("xn", 0)]
        SG = [SG0, gate_bc.bitcast(BF16)]
        SGK = ["SG", "gate_bc"]
        FTq, FTk = cqn[:, 0, :], cqn[:, 1, :]
        ckvn32 = ckvn.bitcast(F32)
        AUGq = ckvn32[:, 0, 0:480].rearrange("p (b h r) -> p b h r", b=NB, h=5)
        AUGk = ckvn32[:, 1, 0:480].rearrange("p (b h r) -> p b h r", b=NB, h=5)
        V5 = fg_bc.bitcast(BF16)[:, :].rearrange("p (b d) -> p b d", b=NB)
        KPEr = slice(96, 128)

        def Vslot(slot):
            return Vg[:, :, slot, :] if slot < 5 else V5

        def VK(slot):
            return ("V", slot) if slot < 5 else "fg_bc"

        scr_rot = Rot([0, 1, 2, 3])
        scrb_rot = Rot([0, 1])
        xt_rot = Rot([0, 1])
        pt_rot = Rot([0, 1, 2, 3])
        wada_rot = Rot([0, 1])

        def mm(out, lhsT, rhs, start, stop, r, x, skip=False):
            if skip:
                P.add("pe", lambda e: e.matmul(out, lhsT=lhsT, rhs=rhs, start=start, stop=stop, skip_group_check=True), r=r, x=x)
            else:
                P.add("pe", lambda e: e.matmul(out, lhsT=lhsT, rhs=rhs, start=start, stop=stop), r=r, x=x)

        def dve(fn, r=(), w=(), x=()):
            P.add("dve", fn, r=r, w=w, x=x)

        def act(fn, r=(), w=(), x=()):
            P.add("act", fn, r=r, w=w, x=x)

        def pool(fn, r=(), w=(), x=()):
            P.add("pool", fn, r=r, w=w, x=x)

        def copy(eng, out, in_, r=(), w=(), x=()):
            P.add(eng, lambda e: e.tensor_copy(out=out, in_=in_), r=r, w=w, x=x)

        def tt_op(eng, out, in0, in1, op, r=(), w=(), x=()):
            P.add(eng, lambda e: e.tensor_tensor(out=out, in0=in0, in1=in1, op=op), r=r, w=w, x=x)

        def stt(eng, out, in0, scalar, in1, op0, op1, r=(), w=(), x=()):
            P.add(eng, lambda e: e.scalar_tensor_tensor(out=out, in0=in0, scalar=scalar, in1=in1, op0=op0, op1=op1), r=r, w=w, x=x)

        def tss(eng, out, in_, scalar, op, r=(), w=(), x=()):
            P.add(eng, lambda e: e.tensor_single_scalar(out=out, in_=in_, scalar=scalar, op=op), r=r, w=w, x=x)

        def ts2(eng, out, in0, s1, s2, op0, op1, r=(), w=(), x=()):
            P.add(eng, lambda e: e.tensor_scalar(out=out, in0=in0, scalar1=s1, scalar2=s2, op0=op0, op1=op1), r=r, w=w, x=x)

        def actf(out, in_, func, r=(), w=(), x=(), **kw):
            P.add("act", lambda e: e.activation(out=out, in_=in_, func=func, **kw), r=r, w=w, x=x)

        def memset(eng, ap, val, w=()):
            P.add(eng, lambda e: e.memset(ap, val), w=w)

        def recip(out, in_, r=(), w=(), x=()):
            P.add("dve", lambda e: e.reciprocal(out=out, in_=in_), r=r, w=w, x=x)

        P.dma("sp", consts[:], dr["consts"], w=["consts"], semkey="c0")
        P.dma("sp", cT[:], dr["cT"], w=["cT"], semkey="c1")
        P.dma("sp", ngt[:].rearrange("p (l c) -> p l c", l=L), dr["ng"].rearrange("l p c -> p l c"), w=["ngt"], semkey="c2")
        P.dma("sp", gqt[:].rearrange("p (l c) -> p l c", l=L), dr["gq"].rearrange("l p c -> p l c"), w=["gqt"], semkey="c3")
        P.dma("sp", gkvt[:].rearrange("p (l c) -> p l c", l=L), dr["gkv"].rearrange("l p c -> p l c"), w=["gkvt"], semkey="c4")
        P.dma("sp", rbc[:].rearrange("p (l c) -> p l c", l=L), dr["rbc"].rearrange("l p c -> p l c"), w=["rbc"], semkey="c5")
        memset("pool", ident[:], 0.0, w=["ident"])
        pool(lambda e: e.affine_select(out=ident[:], in_=ident[:], pattern=[[-1, 128]], compare_op=ALU.not_equal,
                                       fill=1.0, base=0, channel_multiplier=1), r=["ident"], w=["ident"])
        memset("pool", ones_bf[:], 1.0, w=["ones_bf"])
        memset("pool", ones32[:], 1.0, w=["ones32"])
        memset("pool", tri[:], 1.0, w=["tri"])
        pool(lambda e: e.affine_select(out=tri[:], in_=tri[:], pattern=[[1, 128]], compare_op=ALU.is_ge,
                                       fill=0.0, base=0, channel_multiplier=-1), r=["tri"], w=["tri"])
        for t_ in range(4):
            memset("dve", ps2[t_][:], 0.0, w=[PK[2 * t_], PK[2 * t_ + 1]])
        tss("dve", nrbc[:], rbc[:], -1.0, ALU.mult, r=["rbc"], w=["nrbc"])
        memset("pool", Vg[:], 1.0, w=[("V", h) for h in range(5)])
        memset("pool", cmask[:], 1.0, w=["cmask"])
        pool(lambda e: e.affine_select(out=cmask[:], in_=cmask[:], pattern=[[1, 128]], compare_op=ALU.is_ge,
                                       fill=0.0, base=0, channel_multiplier=-1), r=["cmask"], w=["cmask"])

        def adaln_prologue(hook):
            actf(cact[:], cT[:], AF.Silu, r=["cT"], w=["cact"])
            P.dma("act", badaT[:].rearrange("p (l j) -> p l j", l=L), dr["badaT"].rearrange("l p j -> p l j"), w=["badaT"], semkey="c7")
            stg = [(xt[0], ("xt", 0)), (xt[1], ("xt", 1)), (xn[0], ("xn", 0))]
            stg_i = 0
            for l in layers:
                for n in range(24):
                    tile_, tkey = stg[stg_i % 3]
                    wt = stg_i % 2
                    stg_i += 1
                    cs = slice(n * 128, (n + 1) * 128)
                    wv_ = tile_[:].rearrange("p (k n) -> p k n", k=8)
                    P.dma("sp", wv_, dr["wada"][l, n], w=[tkey], semkey=tkey)
                    b = Jnext()
                    if n < 16:
                        for kc in range(8):
                            mm(BK(b, slice(None), 0, 2), wv_[:, kc, :], cact[:, 2 * kc:2 * kc + 2], kc == 0, kc == 7,
                               r=["cact", tkey], x=[PK[b]])
                        for s_ in range(2):
                            col = (l * 2 + s_) * 24 + n
                            tss("dve", modT[:, col:col + 1], BK(b, slice(None), s_, s_ + 1), badaT[:, l * 24 + n:l * 24 + n + 1], ALU.add,
                                r=["badaT"], x=[PK[b]], w=["modT"])
                    else:
                        P.dma("act", badat[wt][:, 0:128], dr["bada"][l][:, cs], w=[("badat", wt)], semkey=("badat", wt))
                        for kc in range(8):
                            mm(BK(b, slice(0, 2), 0, 128), cact[:, 2 * kc:2 * kc + 2], wv_[:, kc, :], kc == 0, kc == 7,
                               r=["cact", tkey], x=[PK[b]])
                        tt_op("dve", modsb[wt][:, 0:128], BK(b, slice(0, 2), 0, 128), badat[wt][:, 0:128], ALU.add,
                              r=[("badat", wt)], x=[PK[b]], w=[("modsb", wt)])
                        P.dma("act", modd[l][:, cs], modsb[wt][:, 0:128], r=[("modsb", wt)], w=[("modd", l)], semkey=("moddw", wt))
                    hook(stg_i)
                for s in range(nseq):
                    o = (l * 2 + s) * 24
                    oa = (l * 2 + s) * 8
                    stt("dve", amod[:, oa:oa + 8], modT[:, o + 8:o + 16], 1.0, ngt[:, l * 8:(l + 1) * 8], ALU.add, ALU.mult,
                        r=["modT", "ngt"], w=["amod"])

        def load_wblock(name, dst_ap, key):
            n = len(WIN_BLOCKS[name])
            src = dr["win"][cur["l"]][:, WIN_OFF[name]:WIN_OFF[name] + 8 * n].rearrange("p (k n) -> p k n", k=8)
            P.dma("pool", dst_ap, src, w=[key], semkey=key)

        def proj(wfn, M, nk, wkeys, rhs_fn, rkeys, ncols=512):
            b = Jnext()
            for kc in range(nk):
                mm(BK(b, slice(0, M), 0, ncols), wfn(kc), rhs_fn(kc), kc == 0, kc == nk - 1, r=list(wkeys) + list(rkeys), x=[PK[b]])
            return b

        def hrhs(tt):
            return lambda kc: hT[:, kc, tt * 512:(tt + 1) * 512]

        CH = 4

        def proj_ms(wfn, M, nk, wkeys, rhs_fn, rkeys, after, ncols=512, chunk=None):
            chunk = chunk or CH
            hold = {}

            def part(k0, k1, lastp):
                if k0 == 0:
                    hold["b"] = Jnext()
                b = hold["b"]
                for kc in range(k0, k1):
                    mm(BK(b, slice(0, M), 0, ncols), wfn(kc), rhs_fn(kc), kc == 0, kc == nk - 1, r=list(wkeys) + list(rkeys), x=[PK[b]])
                if lastp:
                    after(b)
            return [partial(part, k0, min(k0 + chunk, nk), k0 + chunk >= nk) for k0 in range(0, nk, chunk)]

        def rope_tables(s):
            for tt in range(NT):
                rope_tile(s, tt)

        def rope_tile(s, tt):
            C1 = 6.28125
            C2 = float(2 * np.pi - 6.28125)
            if True:
                ts = slice(tt * 512, (tt + 1) * 512)
                P.dma("sp", posi[:], dr["pos"][s][:, ts], w=["posi"], semkey="posi")
                copy("dve", ang[:], posi[:], r=["posi"], w=[("scr", 0)])
                tss("dve", ang[:], ang[:], consts[:, 0:1], ALU.mult, r=[("scr", 0), "consts"], w=[("scr", 0)])
                for which, shift, dst in (("sin", 0.0, SS), ("cos", float(np.pi / 2), CC)):
                    tk = "tab" + which
                    s2i = scr_rot.items[2 + (0 if which == "sin" else 1)]
                    tmp = scr[s2i]
                    tkk = ("scr", s2i)
                    ts2("dve", kf[:], ang[:], shift, float(1.0 / (2 * np.pi)), ALU.add, ALU.mult, r=[("scr", 0)], w=[("scr", 1)])
                    copy("dve", kint[:], kf[:], r=[("scr", 1)], w=["posi"])
                    copy("dve", kf[:], kint[:], r=["posi"], w=[("scr", 1)])
                    stt("dve", tmp[:], kf[:], -C1, ang[:], ALU.mult, ALU.add, r=[("scr", 1), ("scr", 0)], w=[tkk])
                    stt("dve", tmp[:], kf[:], -C2, tmp[:], ALU.mult, ALU.add, r=[("scr", 1), tkk], w=[tkk])
                    if shift != 0.0:
                        tss("dve", tmp[:], tmp[:], shift, ALU.add, r=[tkk], w=[tkk])
                    ts2("dve", tmp[:], tmp[:], 3.14159, -3.14159, ALU.min, ALU.max, r=[tkk], w=[tkk])
                    actf(tmp[:], tmp[:], AF.Sin, r=[tkk], w=[tkk])
                    if which == "sin":
                        tss("dve", SS[64:96, ts], tmp[64:96, :], consts[64:96, 1:2], ALU.mult, r=[tkk, "consts"], w=[tk])
                    else:
                        copy("dve", CC[64:96, ts], tmp[64:96, :], r=[tkk], w=[tk])

        def norm_phase(l, s, xsrc, mid_hook=None):
            oa = (l * 2 + s) * 8
            ob = (l * 2 + s) * 24
            memset("dve", small[:, 0:16], 0.0, w=[("sm", k) for k in range(16)])
            for i in range(NB):
                if i == 3 and mid_hook is not None:
                    mid_hook()
                xi = xt_rot.next()
                P.dma("sp", xt[xi][:], xsrc[s, i * 128:(i + 1) * 128, :], r=[("x1", s, i)], w=[("xt", xi)], semkey=("xt", xi))
                junk = GT[:, 0, 0:1024]
                actf(junk, xt[xi][:], AF.Square, r=[("xt", xi)], w=["GT", ("sm", i)], accum_out=small[:, i:i + 1])
                actf(small[:, 16 + i:17 + i], small[:, i:i + 1], AF.Ln, r=[("sm", i), "consts"], w=[("sm2", i)],
                     bias=consts[:, 2:3], scale=1.0 / D)
                actf(small[:, 32 + i:33 + i], small[:, 16 + i:17 + i], AF.Exp, r=[("sm2", i)], w=[("sm3", i)], scale=-0.5)
                actf(xt[xi][:], xt[xi][:], AF.Copy, r=[("xt", xi), ("sm3", i)], w=[("xt", xi)], scale=small[:, 32 + i:33 + i])
                for half in range(2):
                    b = Jnext()
                    for c4 in range(4):
                        c = half * 4 + c4
                        P.add("pe", lambda e, b=b, c4=c4, c=c, xi=xi: e.transpose(out=BK(b, slice(None), c4 * 128, (c4 + 1) * 128),
                                                                                   in_=xt[xi][:, c * 128:(c + 1) * 128], identity=ident[:]),
                              r=[("xt", xi), "ident"], x=[PK[b]])
                    for c4 in range(4):
                        c = half * 4 + c4
                        ts2("dve", hT[:, c, i * 128:(i + 1) * 128], BK(b, slice(None), c4 * 128, (c4 + 1) * 128),
                            amod[:, oa + c:oa + c + 1], modT[:, ob + c:ob + c + 1], ALU.mult, ALU.add,
                            r=["amod", "modT"], x=[PK[b]], w=["hT"])

        def gate_steps(p):
            wi = 2 + (p % 2)
            sgi = p % 2

            def after(tt, b):
                si = scr_rot.next()
                actf(scr[si][:], BK(b), AF.Exp, x=[PK[b]], w=[("scr", si)], scale=-1.0)
                actf(scr[si][:], scr[si][:], AF.Ln, r=[("scr", si)], w=[("scr", si)], bias=1.0, scale=1.0)
                actf(scr[si][:], scr[si][:], AF.Exp, r=[("scr", si)], w=[("scr", si)], scale=-1.0)
                tt_op("dve", SG[sgi][:, tt * 512:(tt + 1) * 512], BK(b), scr[si][:], ALU.mult,
                      r=[("scr", si)], x=[PK[b]], w=[SGK[sgi]])
            out = []
            for tt in range(NT):
                out += proj_ms(lambda kc: wblk[wi][:, kc, :], 128, 8, [("wblk", wi)], hrhs(tt), ["hT"], partial(after, tt))
            return out

        def attn_steps(g, kind, qi, Krows, scale, vslot, bias_ap=None, bias_keys=(), hB=0):
            pair, par = g // 2, g % 2
            sgt, sgkey = SG[pair % 2], SGK[pair % 2]
            vo = 64 * par
            so = 64 - vo
            Q, K = QT[qi], KT[qi]
            qkey, kkey = QK_[qi], KK_[qi]
            Vs, vkey = Vslot(vslot), VK(vslot)
            groups = []
            for qt in range(NT):
                lst = []
                if kind in ("A", "C"):
                    for j in range(4 * qt + 4):
                        if j < 4 * qt:
                            lst.append((j, 512 * qt, 512 * qt + 512, False))
                        else:
                            lst.append((j, 128 * j, 512 * qt + 512, True))
                else:
                    for m in range(max(0, 4 * qt - 4), 4 * qt + 4):
                        qa = max(128 * m, 512 * qt)
                        qb = min(128 * m + 640, 512 * qt + 512)
                        if qa < qb:
                            lst.append((m, qa, qb, False))
                full = [(qt, n == 0, n == len(lst) - 1) + pc for n, pc in enumerate(lst)]
                for i in range(0, len(full), 2):
                    groups.append(full[i:i + 2])
            st = {}
            obank = {}
            ptof = {}
            fin_pending = []

            def issue_S(gi):
                t = St.next()
                for pi_, (qt, first, last, j, qa, qb, diag) in enumerate(groups[gi]):
                    b = 2 * t + pi_
                    w = qb - qa
                    mm(BK(b, slice(None), 0, w), K[0:Krows, 128 * j:128 * j + 128], Q[0:Krows, qa:qb], True, True,
                       r=[qkey, kkey], x=[PK[b]])
                st[gi] = t

            def exp_fix(gi):
                grp = groups[gi]
                t = st.pop(gi)
                pi = pt_rot.next()
                ptof[gi] = pi
                pk = ("PT", pi)
                ws = [qb - qa for (_, _, _, _, qa, qb, _) in grp]
                kw = dict(scale=scale)
                if bias_ap is not None:
                    kw["bias"] = bias_ap
                xk = [PK[2 * t + i] for i in range(len(grp))]
                if len(grp) == 2:
                    W = max(ws)
                    actf(PT[pi][:].rearrange("p (b c) -> p b c", b=2)[:, :, 0:W],
                         ps2[t][:].rearrange("p (b c) -> p b c", b=2)[:, :, 0:W], AF.Exp, r=list(bias_keys), x=xk, w=[pk], **kw)
                else:
                    actf(PT[pi][:, 0:ws[0]], ps2[t][:, 0:ws[0]], AF.Exp, r=list(bias_keys), x=xk, w=[pk], **kw)
                for pi_, (qt, first, last, j, qa, qb, diag) in enumerate(grp):
                    o = 512 * pi_
                    if kind == "A" and diag:
                        memset("pool", PT[pi][64:128, o:o + 64], 0.0, w=[pk])
                    elif kind == "C" and diag:
                        tt_op("pool", PT[pi][:, o:o + 128], PT[pi][:, o:o + 128], cmask[:], ALU.mult, r=[pk, "cmask"], w=[pk])
                    elif kind == "B":
                        la, lb = qa - 128 * j, qb - 128 * j
                        if la < 256:
                            hi = min(lb, 256)
                            tt_op("pool", PT[pi][:, o:o + hi - la], PT[pi][:, o:o + hi - la], Eb[:, hB, la:hi], ALU.mult,
                                  r=[pk, "Eb"], w=[pk])
                        if lb > 576:
                            lo = max(la, 576)
                            memset("pool", PT[pi][0:64, o + lo - la:o + lb - la], 0.0, w=[pk])

            def do_PV(gi):
                grp = groups[gi]
                pi = ptof.pop(gi)
                pk = ("PT", pi)
                for pi_, (qt, first, last, j, qa, qb, diag) in enumerate(grp):
                    o = 512 * pi_
                    w = qb - qa
                    if first:
                        obank[qt] = Ob.next()
                    ob = obank[qt]
                    c0 = qa - 512 * qt
                    mm(BK(ob, slice(None), c0, c0 + w), Vs[:, j, :], PT[pi][:, o:o + w], first, last, r=[vkey, pk], x=[PK[ob]],
                       skip=True)
                    if last:
                        fin_pending.append((qt, ob))

            def flush_fin():
                while fin_pending:
                    qt, ob = fin_pending.pop(0)
                    s1 = scr_rot.next()
                    s2 = scr_rot.next()
                    cs = slice(qt * 512, (qt + 1) * 512)
                    actf(scr[s1][vo:vo + 64, :], BK(ob, slice(so, so + 64)), AF.Ln, x=[PK[ob]], w=[("scr", s1)])
                    actf(scr[s1][vo:vo + 64, :], scr[s1][vo:vo + 64, :], AF.Exp, r=[("scr", s1)], w=[("scr", s1)], scale=-1.0)
                    tt_op("dve", scr[s2][vo:vo + 64, :], BK(ob, slice(vo, vo + 64)), scr[s1][vo:vo + 64, :], ALU.mult,
                          r=[("scr", s1)], x=[PK[ob]], w=[("scr", s2)])
                    tt_op("dve", GT[vo:vo + 64, pair, cs], scr[s2][vo:vo + 64, :], sgt[vo:vo + 64, cs], ALU.mult,
                          r=[("scr", s2), sgkey], w=["GT"])

            n = len(groups)

            pre_done = {}

            def pre():
                if not pre_done:
                    pre_done[0] = True
                    issue_S(0)

            def step(gi):
                if gi == 0:
                    pre()
                exp_fix(gi)
                if gi >= 1:
                    while deferred:
                        deferred.pop(0)()
                flush_fin()
                if gi + 1 < n:
                    issue_S(gi + 1)
                if gi >= 2:
                    do_PV(gi - 2)

            def tail():
                if n >= 2:
                    do_PV(n - 2)
                do_PV(n - 1)
                deferred.append(flush_fin)
            return [partial(step, gi) for gi in range(n)] + [tail], pre

        def A_loads(l):
            load_wblock("cq", wbig[:, 0:3072].rearrange("p (k n) -> p k n", k=8), "wbig")
            load_wblock("ckv", wbig[:, 3072:5120].rearrange("p (k n) -> p k n", k=8), "wbig")
            load_wblock("kpe", wbig[:, 5120:6656].rearrange("p (k n) -> p k n", k=8), "wbig")
            P.dma("pool", wuq[:], dr["wuq"][l], w=["wuq"], semkey="wuq")
            P.dma("pool", wukv[:], dr["wukv"][l], w=["wukv"], semkey="wukv")

        def A_common(l):
            wcq = wbig[:, 0:3072].rearrange("p (k n) -> p k n", k=8)
            wckv = wbig[:, 3072:5120].rearrange("p (k n) -> p k n", k=8)
            wkpe = wbig[:, 5120:6656].rearrange("p (k n) -> p k n", k=8)
            for tt in range(NT):
                ts = slice(tt * 512, (tt + 1) * 512)
                for (wt, ntile, dstn, dkey, gcol, gkey, nfeat) in ((wcq, 3, cqn, "cqn", gqt, "gqt", 384), (wckv, 2, ckvn, "ckvn", gkvt, "gkvt", 256)):
                    srcs = []
                    for i in range(ntile):
                        b = proj(lambda kc, wt=wt, i=i: wt[:, kc, i * 128:(i + 1) * 128], 128, 8, ["wbig"], hrhs(tt), ["hT"])
                        if i < 2:
                            si = scr_rot.next()
                            tgt, tk = scr[si], ("scr", si)
                        else:
                            tgt, tk = xt[0], ("xt", 0)
                        copy("dve", tgt[:, 0:512], BK(b), x=[PK[b]], w=[tk])
                        srcs.append((tgt, tk))
                    bs = Jnext()
                    for i, (tgt, tk) in enumerate(srcs):
                        bi = scrb_rot.next()
                        actf(scrb[bi][:], tgt[:, 0:512], AF.Square, r=[tk], w=[("scrb", bi)])
                        mm(BK(bs), ones_bf[:], scrb[bi][:], i == 0, i == ntile - 1, r=["ones_bf", ("scrb", bi)], x=[PK[bs]])
                    actf(xn[0][:, 0:512], BK(bs), AF.Ln, r=["consts"], x=[PK[bs]], w=[("xn", 0)], bias=consts[:, 2:3], scale=1.0 / nfeat)
                    actf(xn[0][:, 0:512], xn[0][:, 0:512], AF.Exp, r=[("xn", 0)], w=[("xn", 0)], scale=-0.5)
                    for i, (tgt, tk) in enumerate(srcs):
                        stt("dve", dstn[:, i, ts], tgt[:, 0:512], gcol[:, l * ntile + i:l * ntile + i + 1], xn[0][:, 0:512], ALU.mult, ALU.mult,
                            r=[tk, gkey, ("xn", 0)], w=[dkey])
                b1 = proj(lambda kc: wkpe[:, kc, 0:96], 96, 8, ["wbig"], hrhs(tt), ["hT"])
                b2 = proj(lambda kc: wkpe[:, kc, 96:192], 96, 8, ["wbig"], hrhs(tt), ["hT"])
                s1 = scr_rot.next()
                s2 = scr_rot.next()
                tt_op("dve", scr[s1][64:96, :], BK(b1, slice(64, 96)), CC[64:96, ts], ALU.mult, r=["tabcos"], x=[PK[b1]], w=[("scr", s1)])
                tt_op("dve", scr[s2][64:96, :], BK(b2, slice(64, 96)), SS[64:96, ts], ALU.mult, r=["tabsin"], x=[PK[b2]], w=[("scr", s2)])
                tt_op("dve", CC[KPEr, ts], scr[s1][64:96, :], scr[s2][64:96, :], ALU.add, r=[("scr", s1), ("scr", s2)], w=["KPE"])

        def A_head_steps(l, h, qi):
            g = h
            steps = []
            vslot = h
            Vs, vkey = Vslot(vslot), VK(vslot)

            def q_ms(tt):
                ts = slice(tt * 512, (tt + 1) * 512)
                rf = lambda kc: cqn[:, kc, ts]
                hold = {}

                def after1(b1):
                    copy("dve", QT[qi][0:64, ts], BK(b1, slice(0, 64)), x=[PK[b1]], w=[QK_[qi]])
                    s1 = scr_rot.next()
                    hold["s1"] = s1
                    tt_op("dve", scr[s1][64:96, :], BK(b1, slice(64, 96)), CC[64:96, ts], ALU.mult, r=["tabcos"], x=[PK[b1]], w=[("scr", s1)])

                def after2(b2):
                    s1 = hold["s1"]
                    s2 = scr_rot.next()
                    tt_op("dve", scr[s2][64:96, :], BK(b2, slice(64, 96)), SS[64:96, ts], ALU.mult, r=["tabsin"], x=[PK[b2]], w=[("scr", s2)])
                    tt_op("pool", QT[qi][64:96, ts], scr[s1][64:96, :], scr[s2][64:96, :], ALU.add, r=[("scr", s1), ("scr", s2)], w=[QK_[qi]])
                return (proj_ms(lambda kc: wuq[:, kc, 192 * h:192 * h + 96], 96, 3, ["wuq"], rf, ["cqn"], after1, chunk=3)
                        + proj_ms(lambda kc: wuq[:, kc, 192 * h + 96:192 * h + 192], 96, 3, ["wuq"], rf, ["cqn"], after2, chunk=3))

            def k_ms(tt):
                ts = slice(tt * 512, (tt + 1) * 512)

                def after(b3):
                    copy("dve", KT[qi][0:64, ts], BK(b3, slice(0, 64)), x=[PK[b3]], w=[KK_[qi]])
                    copy("dve", KT[qi][64:96, ts], CC[KPEr, ts], r=["KPE"], w=[KK_[qi]])
                return proj_ms(lambda kc: wukv[:, kc, 64 * h:64 * h + 64], 64, 2, ["wukv"], lambda kc: ckvn[:, kc, ts], ["ckvn"], after)

            def v_ms(half):
                hold = {}

                def part(q):
                    if q == 0:
                        hold["b"] = Jnext()
                    b = hold["b"]
                    for bb in range(8):
                        blk = half * 8 + bb
                        for kc in range(2):
                            mm(BK(b, slice(None), bb * 64, (bb + 1) * 64), ckvn[:, kc, blk * 128:(blk + 1) * 128],
                               wukv[:, kc, 384 + 64 * h:384 + 64 * h + 64], kc == 0, kc == 1, r=["ckvn", "wukv"], x=[PK[b]], skip=True)
                    if True:
                        vo_ = 64 * (g % 2)
                        copy("dve", Vs[:, half * 8:half * 8 + 8, vo_:vo_ + 64], BK(b).rearrange("p (b d) -> p b d", b=8), x=[PK[b]], w=[vkey])
                return [partial(part, 0)]

            for tt in range(NT):
                steps += q_ms(tt)
                steps += k_ms(tt)
            steps += v_ms(0)
            steps += v_ms(1)
            return steps

        def slotBC(kind, h):
            return h if kind == "B" else (h + 5) % 6

        def v_batch_steps(kind, blockname, ncols, with_f, g0):
            wv = wbig[:, 0:8 * ncols].rearrange("p (k n) -> p k n", k=8)

            def after(blk, b):
                for h in range(5):
                    vo_ = 64 * ((g0 + h) % 2)
                    sl = slotBC(kind, h)
                    copy("dve", Vslot(sl)[:, blk, vo_:vo_ + 64], BK(b, slice(None), 64 * h, 64 * h + 64), x=[PK[b]], w=[VK(sl)])
                if with_f:
                    copy("dve", ZF[:, blk * 5:blk * 5 + 5], BK(b, slice(None), 320, 325), x=[PK[b]], w=["ZF"])
            out = []
            for blk in range(NB):
                out += proj_ms(lambda kc, blk=blk: hT[:, kc, blk * 128:(blk + 1) * 128], 128, 8, ["hT"], lambda kc: wv[:, kc, :], ["wbig"],
                               partial(after, blk), ncols=ncols)
            return out

        def qk_steps(g, qi):
            wi = g % 2

            def after(tt, b):
                ts = slice(tt * 512, (tt + 1) * 512)
                copy("dve", QT[qi][0:64, ts], BK(b, slice(0, 64)), x=[PK[b]], w=[QK_[qi]])
                copy("dve", KT[qi][0:64, ts], BK(b, slice(64, 128)), x=[PK[b]], w=[KK_[qi]])
            out = []
            for tt in range(NT):
                out += proj_ms(lambda kc: wblk[wi][:, kc, :], 128, 8, [("wblk", wi)], hrhs(tt), ["hT"], partial(after, tt))
            return out

        def B_common_steps(l):
            def eb():
                for h in range(5):
                    si = scr_rot.next()
                    P.dma("sp", scr[si][:, 0:256], dr["rbt"][l, h], w=[("scr", si)], semkey=("rbt", si))
                    actf(Eb[:, h, :], scr[si][:, 0:256], AF.Exp, r=[("scr", si), "nrbc"], w=["Eb"], bias=nrbc[:, l * 5 + h:l * 5 + h + 1], scale=1.0)
                memset("pool", Eb[64:128, :, 0:64], 0.0, w=["Eb"])
            return [eb] + v_batch_steps("B", "bv", 320, False, 6)

        def C_common_steps(l):
            def fprep():
                tt_op("dve", ZF[:], ZF[:], cfbt[:], ALU.add, r=["ZF", "cfbt"], w=["ZF"])
                actf(ZF[:], ZF[:], AF.Exp, r=["ZF"], w=["ZF"], scale=-1.0)
                actf(ZF[:], ZF[:], AF.Ln, r=["ZF"], w=["ZF"], bias=1.0, scale=1.0)
                b1 = Jnext()
                mm(BK(b1, slice(None), 0, 80), tri[:], ZF[:], True, True, r=["tri", "ZF"], x=[PK[b1]])
                b2 = Jnext()
                mm(BK(b2, slice(None), 0, 80), ones32[:], ZF[:], True, True, r=["ones32", "ZF"], x=[PK[b2]])
                copy("dve", Wt32[:], BK(b2, slice(None), 0, 80), x=[PK[b2]], w=["Wt32"])
                tot = Wt32[:].rearrange("p (b h) -> p h b", h=5)
                pre = Wc[:].rearrange("p (b h) -> p h b", h=5)
                memset("dve", Wc[:], 0.0, w=["Wc"])
                for h in range(5):
                    dve(lambda e, h=h: e.tensor_tensor_scan(out=pre[:, h, 1:NB], data0=ones32[:, 0:NB - 1], data1=tot[:, h, 0:NB - 1],
                                                            initial=0.0, op0=ALU.mult, op1=ALU.add), r=["Wt32", "ones32"], w=["Wc"])
                tt_op("dve", Wc[:], BK(b1, slice(None), 0, 80), Wc[:], ALU.add, r=["Wc"], x=[PK[b1]], w=["Wc"])
                tss("dve", Wc[:], Wc[:], 8.0, ALU.mult, r=["Wc"], w=["Wc"])
                memset("dve", AUGq, 1.0, w=["ckvn"])
                memset("dve", AUGk, 1.0, w=["ckvn"])
                Wv = Wc[:].rearrange("p (b h) -> p b h", h=5)
                Hv = Whi[:].rearrange("p (b h) -> p b h", h=5)
                for r_ in range(3):
                    copy("dve", Whi[:], Wc[:], r=["Wc"], w=["Whi"])
                    copy("dve", AUGk[:, :, :, 3 + r_], Hv, r=["Whi"], w=["ckvn"])
                    tss("dve", AUGq[:, :, :, r_], Hv, -1.0, ALU.mult, r=["Whi"], w=["ckvn"])
                    if r_ < 2:
                        tt_op("dve", Wv, Wv, AUGk[:, :, :, 3 + r_], ALU.subtract, r=["Wc", "ckvn"], w=["Wc"])
                for (AUG, FT) in ((AUGq, FTq), (AUGk, FTk)):
                    for q4 in range(4):
                        b = Jnext()
                        for bb in range(4):
                            blk = q4 * 4 + bb
                            P.add("pe", lambda e, b=b, bb=bb, blk=blk, AUG=AUG: e.transpose(
                                out=BK(b, slice(0, 30), bb * 128, (bb + 1) * 128), in_=AUG[:, blk, :, :].rearrange("p h r -> p (h r)"),
                                identity=ident[:]), r=["ckvn", "ident"], x=[PK[b]])
                        copy("dve", FT[0:30, q4 * 512:(q4 + 1) * 512], BK(b, slice(0, 30)), x=[PK[b]], w=["cqn"])
            return v_batch_steps("C", "cv", 325, True, 11) + [fprep]

        def C_rows_step(h, qi):
            def f():
                P.dma("sp", QT[qi][64:70, :], FTq[6 * h:6 * h + 6, :], r=["cqn"], w=[QK_[qi]], semkey=("ftq", qi))
                P.dma("sp", KT[qi][64:70, :], FTk[6 * h:6 * h + 6, :], r=["cqn"], w=[KK_[qi]], semkey=("ftk", qi))
            return f

        def head_kind(g):
            return "A" if g < 6 else ("B" if g < 11 else "C")

        def prefetch_for(l, g):
            def f():
                if g >= 16:
                    return
                k = head_kind(g)
                if g % 2 == 0:
                    load_wblock("gate%d" % (g // 2), wblk[2 + (g // 2) % 2][:], ("wblk", 2 + (g // 2) % 2))
                if k == "B":
                    load_wblock("bqk%d" % (g - 6), wblk[g % 2][:], ("wblk", g % 2))
                    if g == 6:
                        load_wblock("bv", wbig[:, 0:8 * 320].rearrange("p (k n) -> p k n", k=8), "wbig")
                elif k == "C":
                    load_wblock("cqk%d" % (g - 11), wblk[g % 2][:], ("wblk", g % 2))
                    if g == 11:
                        load_wblock("cv", wbig[:, 0:8 * 325].rearrange("p (k n) -> p k n", k=8), "wbig")
                        P.dma("sp", cfbt[:], dr["cfb"][l], w=["cfbt"], semkey="cfbt")
                if g == 13:
                    P.dma("pool", wbig[:].rearrange("p (c n) -> p c n", c=8), dr["wout"][l], w=["wbig"], semkey="wbig")
            return f

        def prep_steps(l, g):
            qi = g % 2
            k = head_kind(g)
            steps = []
            if k == "B" and g == 6:
                steps += B_common_steps(l)
            if k == "C" and g == 11:
                steps += C_common_steps(l)
            if g % 2 == 0:
                steps += gate_steps(g // 2)
            if k == "A":
                steps += A_head_steps(l, g, qi)
            else:
                steps += qk_steps(g, qi)
                if k == "C":
                    steps.append(C_rows_step(g - 11, qi))
            steps.insert(len(steps) // 2, prefetch_for(l, g + 1))
            return steps

        def attn_for(l, g):
            qi = g % 2
            k = head_kind(g)
            if k == "A":
                return attn_steps(g, "A", qi, 96, float(96 ** -0.5), g)
            if k == "B":
                h = g - 6
                return attn_steps(g, "B", qi, 64, 0.125, slotBC("B", h), bias_ap=rbc[:, l * 5 + h:l * 5 + h + 1], bias_keys=("rbc",), hB=h)
            h = g - 11
            return attn_steps(g, "C", qi, 70, 0.125, slotBC("C", h))

        deferred = []

        def interleave(main, side, before_last=None):
            nm, nsd = len(main), len(side)
            nme = max(1, int(nm * 0.7))
            done = 0
            for i, st_ in enumerate(main):
                if i == nm - 1 and before_last is not None:
                    while done < nsd:
                        side[done]()
                        done += 1
                    before_last()
                st_()
                tgt = min(nsd, (nsd * (i + 1)) // nme)
                while done < tgt:
                    side[done]()
                    done += 1
            while done < nsd:
                side[done]()
                done += 1

        def out_phase(l, s, xsrc, xdst, do_final):
            wo = wbig[:].rearrange("p (c n) -> p c n", c=8)
            P.dma("sp", gate_bc[:], modd[l, s:s + 1, 2 * D:3 * D].partition_broadcast(128), r=[("modd", l)], w=["gate_bc"], semkey="gate_bc")
            if do_final:
                P.dma("sp", fg_bc[:], dr["fg"], w=["fg_bc"], semkey="fg_bc")
                memset("dve", small[:, 0:16], 0.0, w=[("sm", k) for k in range(16)])
            obufs = [(xt[0], ("xt", 0)), (xt[1], ("xt", 1)), (xn[0], ("xn", 0))]
            for i in range(NB):
                xb, xk = obufs[i % 3]
                P.dma("sp", xb[:], xsrc[s, i * 128:(i + 1) * 128, :], r=[("x1", s, i)], w=[xk], semkey=xk)
                for half in range(2):
                    b = Jnext()
                    for c in range(8):
                        mm(BK(b), GT[:, c, i * 128:(i + 1) * 128], wo[:, c, half * 512:(half + 1) * 512], c == 0, c == 7,
                           r=["GT", "wbig"], x=[PK[b]])
                    hs = slice(half * 512, (half + 1) * 512)
                    si = scr_rot.next()
                    tt_op("dve", scr[si][:], BK(b), gate_bc[:, hs], ALU.mult, r=["gate_bc"], x=[PK[b]], w=[("scr", si)])
                    tt_op("pool", xb[:, hs], scr[si][:], xb[:, hs], ALU.add, r=[("scr", si), xk], w=[xk])
                if do_final:
                    junk = hT[:, 0, 0:1024]
                    actf(junk, xb[:], AF.Square, r=[xk], w=["hT", ("sm", i)], accum_out=small[:, i:i + 1])
                    actf(small[:, 16 + i:17 + i], small[:, i:i + 1], AF.Ln, r=[("sm", i), "consts"], w=[("sm2", i)],
                         bias=consts[:, 2:3], scale=1.0 / D)
                    actf(small[:, 32 + i:33 + i], small[:, 16 + i:17 + i], AF.Exp, r=[("sm2", i)], w=[("sm3", i)], scale=-0.5)
                    stt("dve", xb[:], xb[:], small[:, 32 + i:33 + i], fg_bc[:], ALU.mult, ALU.mult,
                        r=[xk, ("sm3", i), "fg_bc"], w=[xk])
                    P.dma("act", xdst[s, i * 128:(i + 1) * 128, :], xb[:], r=[xk], w=[("outd", s, i)], semkey=("xo", i % 3))
                else:
                    P.dma("act", xdst[s, i * 128:(i + 1) * 128, :], xb[:], r=[xk],
                          w=[("x1", s, i) if xdst is x1_d else ("outd", s, i)], semkey=("xo", i % 3))

        nl = len(layers)

        _rope_done = set()

        def _hook(k):
            if k in (6, 14, 22, 30):
                _rope_done.add((k - 6) // 8)
                rope_tile(0, (k - 6) // 8)
        adaln_prologue(_hook)
        for tt_ in range(NT):
            if tt_ not in _rope_done:
                rope_tile(0, tt_)
        for s in range(nseq):
            cur["J"] = Jw
            memset("pool", V5, 1.0, w=["fg_bc"])
            if s > 0:
                rope_tables(s)
            for li, l in enumerate(layers):
                cur["l"], cur["s"] = l, s
                last = li == nl - 1
                xsrc = dr["x"] if li == 0 else x1_d
                xdst = out_d if last else x1_d
                cur["J"] = Jw
                def _mid(l=l):
                    A_loads(l)
                    prefetch_for(l, 0)()
                norm_phase(l, s, xsrc, mid_hook=_mid)
                A_common(l)
                for st_ in prep_steps(l, 0):
                    st_()
                cur["J"] = Jn
                cur_att = attn_for(l, 0)
                for g in range(16):
                    side = prep_steps(l, g + 1) if g < 15 else []
                    nxt = attn_for(l, g + 1) if g < 15 else None
                    interleave(cur_att[0], side, before_last=(nxt[1] if nxt is not None else None))
                    cur_att = nxt
                while deferred:
                    deferred.pop(0)()
                cur["J"] = Jw
                out_phase(l, s, xsrc, xdst, do_final=(last and final))
        P.sbuf_free = nc.sbuf_bytes_remaining
        P.emit()
    return nc, P


_CACHE = {}


def _get_prog(key):
    if key not in _CACHE:
        _CACHE[key] = build_program(*key)
    return _CACHE[key]


def kernel(x, c, positions, w_ada, b_ada, norm_g, w_in, a_q_norm_g, a_w_uq, a_kv_norm_g, a_w_ukv,
           b_rel_bias, c_forget_b, w_out, final_g):
    sh = _prep_shared(w_ada, b_ada, norm_g, w_in, a_q_norm_g, a_w_uq, a_kv_norm_g, a_w_ukv, b_rel_bias,
                      c_forget_b, w_out, final_g)
    x = np.asarray(x, np.float32)
    c = np.asarray(c, np.float32)
    positions = np.asarray(positions, np.int32)
    in_maps = []
    for i in range(NCORES):
        m = dict(sh)
        m["x"] = np.ascontiguousarray(x[2 * i:2 * i + 2])
        cc = c[2 * i:2 * i + 2]
        m["cT"] = np.ascontiguousarray(cc.reshape(2, 8, 128).transpose(2, 1, 0).reshape(128, 16))
        m["pos"] = np.ascontiguousarray(np.broadcast_to(positions[2 * i:2 * i + 2, None, :], (2, 128, T)))
        in_maps.append(m)
    nc, _ = _get_prog(((0, 1), True, 2))
    res = run_bass_kernel_spmd(nc, in_maps, core_ids=list(range(NCORES)))
    out = np.concatenate([np.asarray(r["out"]) for r in res.results], axis=0)
    return out.astype(np.float32)
```
